# Optimizing a Trainium2 kernel written in Bass

```python
import jax, jax.numpy as jnp
from jax import lax
import numpy as np

D_MODEL = 2048
BATCH = 4
SEQ = 2048
DEPTH = 1
DEC_BATCH = 128
DEC_SEQ = 4
PAST_LEN = 16384
PAGE_SIZE = 128

CHUNK = 128
D_A = D_MODEL // 2
G_A = 8
D_B = D_MODEL // 2
CONV_B = 31
N_MEM = 256
N_XHEADS = 4
XHEAD_DIM = D_MODEL // 8
D_C = N_XHEADS * XHEAD_DIM
D_FF = ((8 * D_MODEL // 3 + 127) // 128) * 128
CONV_F = 3
N_BRANCH = 3
D_IN = 2 * D_A + 2 * D_B + D_C + N_BRANCH * D_MODEL
EPS = 1e-6

kernel_name = 'gated_gmlp_conformer_xattn_decoder_step'


def rms_norm(x, g):
    xf = x.astype(jnp.float32)
    y = xf * lax.rsqrt(jnp.mean(xf * xf, axis=-1, keepdims=True) + EPS)
    return (y * g.astype(jnp.float32)).astype(x.dtype)


def layer_norm(x, g, b):
    xf = x.astype(jnp.float32)
    mu = jnp.mean(xf, axis=-1, keepdims=True)
    var = jnp.mean(jnp.square(xf - mu), axis=-1, keepdims=True)
    y = (xf - mu) * lax.rsqrt(var + EPS)
    return (y * g.astype(jnp.float32) + b.astype(jnp.float32)).astype(x.dtype)


def causal_dwconv(x, hist, w, b):
    xc = jnp.concatenate([hist, x], axis=1)
    y = lax.conv_general_dilated(xc, w[:, None, :].astype(xc.dtype), (1,), 'VALID',
                                 dimension_numbers=('NWC', 'WIO', 'NWC'),
                                 feature_group_count=x.shape[-1])
    return y + b, xc[:, -(w.shape[0] - 1):]


def chunk_spatial_gate(v, w_s, b_s):
    B, L, _ = v.shape
    c = min(L, CHUNK)
    n = L // c
    vr = v.reshape(B, n, c, G_A, D_A // G_A)
    mask = jnp.tril(jnp.ones((c, c), dtype=bool))
    wm = jnp.where(mask, w_s[:, :c, :c], 0)
    s = jnp.einsum('gts,bnsgd->bntgd', wm, vr) + b_s[:, :c].T[None, None, :, :, None]
    return s.reshape(B, L, D_A)


def cross_attend(q, k, v):
    B, L = q.shape[0], q.shape[1]
    s = jnp.einsum('blhe,bmhe->bhlm', q, k).astype(jnp.float32) * (XHEAD_DIM ** -0.5)
    p = jax.nn.softmax(s, axis=-1).astype(v.dtype)
    o = jnp.einsum('bhlm,bmhe->blhe', p, v)
    return o.reshape(B, L, D_C)


def memory_kv(mem, g_mem, w_k, w_v):
    B = mem.shape[0]
    m = rms_norm(mem, g_mem)
    k = (m @ w_k).reshape(B, N_MEM, N_XHEADS, XHEAD_DIM)
    v = (m @ w_v).reshape(B, N_MEM, N_XHEADS, XHEAD_DIM)
    return k, v


def decoder_layer(x, mem_k, mem_v, conv_hist, ffn_hist, g_mix, w_in, ln_v_g, ln_v_b, w_s, b_s, w_pa,
                  conv_w, conv_b, ln_b_g, ln_b_b, w_pb, w_pc, w_o, g_ffn, w_up, ffn_conv_w,
                  ffn_conv_b, w_down):
    B, L, _ = x.shape
    h = rms_norm(x, g_mix)
    proj = h @ w_in
    zA, zB, q, gates = jnp.split(proj, [2 * D_A, 2 * D_A + 2 * D_B, 2 * D_A + 2 * D_B + D_C], axis=-1)
    u, v = jnp.split(jax.nn.gelu(zA), 2, axis=-1)
    v = layer_norm(v, ln_v_g, ln_v_b)
    a_out = (u * chunk_spatial_gate(v, w_s, b_s)) @ w_pa
    glu_a, glu_b = jnp.split(zB, 2, axis=-1)
    cv, new_conv = causal_dwconv(glu_a * jax.nn.sigmoid(glu_b), conv_hist, conv_w, conv_b)
    b_out = jax.nn.silu(layer_norm(cv, ln_b_g, ln_b_b)) @ w_pb
    c_out = cross_attend(q.reshape(B, L, N_XHEADS, XHEAD_DIM), mem_k, mem_v) @ w_pc
    g_a, g_b, g_c = jnp.split(jax.nn.sigmoid(gates), N_BRANCH, axis=-1)
    x = x + (g_a * a_out + g_b * b_out + g_c * c_out) @ w_o
    f_a, f_g = jnp.split(rms_norm(x, g_ffn) @ w_up, 2, axis=-1)
    f_c, new_ffn = causal_dwconv(f_a, ffn_hist, ffn_conv_w, ffn_conv_b)
    x = x + (jax.nn.gelu(f_c) * f_g) @ w_down
    return x, new_conv, new_ffn, v


def setup_inputs(seed: int = 0) -> dict:
    key = jax.random.key(seed)
    ks = jax.random.split(key, 40)
    f32 = jnp.float32

    def nrm(k, shape, scale=1.0):
        return jax.random.normal(k, shape, f32) * scale

    return {
        'x_prompt': nrm(ks[0], (BATCH, SEQ, D_MODEL)),
        'x_sample': nrm(ks[1], (DEC_BATCH, DEC_SEQ, D_MODEL)),
        'mem_prompt': nrm(ks[2], (BATCH, N_MEM, D_MODEL)),
        'cache_mem_k': nrm(ks[3], (DEPTH, DEC_BATCH, N_MEM, N_XHEADS, XHEAD_DIM)),
        'cache_mem_v': nrm(ks[4], (DEPTH, DEC_BATCH, N_MEM, N_XHEADS, XHEAD_DIM)),
        'state_conv': nrm(ks[5], (DEPTH, DEC_BATCH, CONV_B - 1, D_B), 0.5),
        'state_ffn_conv': nrm(ks[6], (DEPTH, DEC_BATCH, CONV_F - 1, D_FF), 0.5),
        'g_mix': 1.0 + nrm(ks[7], (DEPTH, D_MODEL), 0.02),
        'w_in': nrm(ks[8], (DEPTH, D_MODEL, D_IN), D_MODEL ** -0.5),
        'ln_v_g': 1.0 + nrm(ks[9], (DEPTH, D_A), 0.02),
        'ln_v_b': nrm(ks[10], (DEPTH, D_A), 0.02),
        'w_s': nrm(ks[11], (DEPTH, G_A, CHUNK, CHUNK), CHUNK ** -0.5),
        'b_s': 1.0 + nrm(ks[12], (DEPTH, G_A, CHUNK), 0.02),
        'w_pa': nrm(ks[13], (DEPTH, D_A, D_MODEL), D_A ** -0.5),
        'conv_w': nrm(ks[14], (DEPTH, CONV_B, D_B), CONV_B ** -0.5),
        'conv_b': nrm(ks[15], (DEPTH, D_B), 0.02),
        'ln_b_g': 1.0 + nrm(ks[16], (DEPTH, D_B), 0.02),
        'ln_b_b': nrm(ks[17], (DEPTH, D_B), 0.02),
        'w_pb': nrm(ks[18], (DEPTH, D_B, D_MODEL), D_B ** -0.5),
        'g_mem': 1.0 + nrm(ks[19], (DEPTH, D_MODEL), 0.02),
        'w_k': nrm(ks[20], (DEPTH, D_MODEL, D_C), D_MODEL ** -0.5),
        'w_v': nrm(ks[21], (DEPTH, D_MODEL, D_C), D_MODEL ** -0.5),
        'w_pc': nrm(ks[22], (DEPTH, D_C, D_MODEL), D_C ** -0.5),
        'w_o': nrm(ks[23], (DEPTH, D_MODEL, D_MODEL), D_MODEL ** -0.5),
        'g_ffn': 1.0 + nrm(ks[24], (DEPTH, D_MODEL), 0.02),
        'w_up': nrm(ks[25], (DEPTH, D_MODEL, 2 * D_FF), D_MODEL ** -0.5),
        'ffn_conv_w': nrm(ks[26], (DEPTH, CONV_F, D_FF), CONV_F ** -0.5),
        'ffn_conv_b': nrm(ks[27], (DEPTH, D_FF), 0.02),
        'w_down': nrm(ks[28], (DEPTH, D_FF, D_MODEL), D_FF ** -0.5),
        'g_final': 1.0 + nrm(ks[29], (D_MODEL,), 0.02),
    }


def reference(x_prompt, x_sample, mem_prompt, cache_mem_k, cache_mem_v, state_conv, state_ffn_conv,
              g_mix, w_in, ln_v_g, ln_v_b, w_s, b_s, w_pa, conv_w, conv_b, ln_b_g, ln_b_b, w_pb,
              g_mem, w_k, w_v, w_pc, w_o, g_ffn, w_up, ffn_conv_w, ffn_conv_b, w_down, g_final):
    hp, hs = x_prompt, x_sample
    Bp = x_prompt.shape[0]
    mk_p, mv_p, cv_p, ff_p, cv_s, ff_s, v_s = [], [], [], [], [], [], []
    for l in range(DEPTH):
        lw = (g_mix[l], w_in[l], ln_v_g[l], ln_v_b[l], w_s[l], b_s[l], w_pa[l], conv_w[l], conv_b[l],
              ln_b_g[l], ln_b_b[l], w_pb[l], w_pc[l], w_o[l], g_ffn[l], w_up[l], ffn_conv_w[l],
              ffn_conv_b[l], w_down[l])
        k_p, v_p = memory_kv(mem_prompt, g_mem[l], w_k[l], w_v[l])
        zc = jnp.zeros((Bp, CONV_B - 1, D_B), hp.dtype)
        zf = jnp.zeros((Bp, CONV_F - 1, D_FF), hp.dtype)
        hp, c_p, f_p, _ = decoder_layer(hp, k_p, v_p, zc, zf, *lw)
        hs, c_s, f_s, vr_s = decoder_layer(hs, cache_mem_k[l], cache_mem_v[l], state_conv[l],
                                           state_ffn_conv[l], *lw)
        mk_p.append(k_p); mv_p.append(v_p); cv_p.append(c_p); ff_p.append(f_p)
        cv_s.append(c_s); ff_s.append(f_s); v_s.append(vr_s)
    y_prompt = rms_norm(hp, g_final)
    y_sample = rms_norm(hs, g_final)
    return (y_prompt, y_sample, jnp.stack(mk_p), jnp.stack(mv_p), jnp.stack(cv_p), jnp.stack(ff_p),
            jnp.stack(cv_s), jnp.stack(ff_s), jnp.stack(v_s))
```

```python
import numpy as np
import ml_dtypes
from contextlib import ExitStack
import concourse.bass as bass
import concourse.mybir as mybir
from concourse.bass_utils import run_bass_kernel_spmd

F32 = mybir.dt.float32
BF16 = mybir.dt.bfloat16
F32R = mybir.dt.float32r
AF = mybir.ActivationFunctionType
ALU = mybir.AluOpType
AX = mybir.AxisListType

ENGS = ("pe", "act", "dve", "pool", "sp")
D = 2048
DA = 1024
DFF = 5504
NJ = 43
EPS = 1e-6
NCORES = 8


class Op:
    __slots__ = ("eng", "fn", "reads", "writes", "dma", "waits", "sig", "signal", "idx")

    def __init__(self, eng, fn, reads, writes, dma):
        self.eng = eng
        self.fn = fn
        self.reads = reads
        self.writes = writes
        self.dma = dma
        self.waits = []
        self.sig = None
        self.signal = False


class Prog:
    def __init__(self, nc, stack):
        self.nc = nc
        self.stack = stack
        self.ops = []
        self.sems = {}
        self.witems = []
        self.lookahead = 3

    def sem(self, key):
        if key not in self.sems:
            self.sems[key] = self.stack.enter_context(self.nc.semaphore("s_" + key))
        return self.sems[key]

    def op(self, eng, fn, reads=(), writes=(), dma=None):
        o = Op(eng, fn, tuple(reads), tuple(writes), dma)
        self.ops.append(o)
        return o

    def wdma(self, eng, fn, writes, dma):
        o = Op(eng, fn, (), tuple(writes), dma)
        self.witems.append((o, len(self.ops)))

    def finalize(self):
        ins_at = {}
        for j, (o, pos) in enumerate(self.witems):
            p = self.witems[max(0, j - self.lookahead)][1]
            ins_at.setdefault(p, []).append(o)
        merged = []
        for i, o in enumerate(self.ops):
            merged.extend(ins_at.get(i, ()))
            merged.append(o)
        merged.extend(ins_at.get(len(self.ops), ()))
        self.ops = merged
        last_w = {}
        readers = {}
        deps_all = []
        for i, o in enumerate(self.ops):
            deps = {}
            for r in o.reads:
                w = last_w.get(r)
                if w is not None:
                    deps[w] = "RAW"
            for r in o.writes:
                w = last_w.get(r)
                if w is not None and w not in deps:
                    deps[w] = "WAW"
                for rd in readers.get(r, ()):
                    if rd not in deps:
                        deps[rd] = "WAR"
            deps.pop(i, None)
            keep = {}
            for d, kind in deps.items():
                p = self.ops[d]
                if p.dma is None and o.dma is None and p.eng == o.eng:
                    if o.eng == "pe":
                        continue
                if p.dma is not None:
                    keep[("dma", p.dma, d)] = d
                else:
                    k = ("eng", p.eng)
                    if k not in keep or keep[k] < d:
                        keep[k] = d
            dl = sorted(set(keep.values()))
            deps_all.append(dl)
            for d in dl:
                self.ops[d].signal = True
            for r in o.reads:
                readers.setdefault(r, []).append(i)
            for r in o.writes:
                last_w[r] = i
                readers[r] = []
        cnt = {}
        for o in self.ops:
            if o.dma is not None:
                k = "d_" + o.dma
                cnt[k] = cnt.get(k, 0) + 16
                o.sig = (k, cnt[k])
            elif o.signal:
                k = "e_" + o.eng
                cnt[k] = cnt.get(k, 0) + 1
                o.sig = (k, cnt[k])
        self.final_cnt = cnt
        seen = {e: {} for e in ENGS}
        for o, dl in zip(self.ops, deps_all):
            need = {}
            for d in dl:
                k, v = self.ops[d].sig
                if k.startswith("d_const"):
                    v = cnt[k]
                if v > need.get(k, 0):
                    need[k] = v
            s = seen[o.eng]
            for k, v in need.items():
                if s.get(k, 0) >= v:
                    continue
                s[k] = v
                o.waits.append((k, v))

    def emit(self):
        nc = self.nc
        for k in self.final_cnt:
            self.sem(k)
        per = {e: [o for o in self.ops if o.eng == e] for e in ENGS}
        names = {"pe": "tensor", "act": "scalar", "dve": "vector", "pool": "gpsimd", "sp": "sync"}
        with nc.Block() as block:
            for e in ENGS:
                if not per[e]:
                    continue

                def body(eng, lst=per[e]):
                    for o in lst:
                        for k, v in o.waits:
                            eng.wait_ge(self.sems[k], v)
                        if o.fn is None:
                            continue
                        ins = o.fn(eng)
                        if o.sig is not None:
                            ins.then_inc(self.sems[o.sig[0]], 16 if o.dma is not None else 1)

                getattr(block, names[e])(body)


CV_GMIX, CV_GFFN, CV_GMEM = 0, 16, 32
CV_LNBG, CV_LNBB, CV_CONVB = 48, 56, 64
CV_CONVW = 72
CV_FFW = 320
CV_FFB = 449
CV_ROWS = 512

SKIP = set()
PASSES = [
    ([0, 1, 2], True),
    ([3, 4, 5], False),
    ([6, 7, 8], False),
]


def build_nc(passes=PASSES, debug=False, stop=None):
    nc = bass.Bass("TRN2", target_bir_lowering=False)

    def din(n, s, d=F32):
        return nc.dram_tensor(n, list(s), d, kind="ExternalInput")

    def dout(n, s, d=F32):
        return nc.dram_tensor(n, list(s), d, kind="ExternalOutput").ap()

    xp = din("xp", [9 * 128, D]).ap()
    xsm = din("xsm", [64, D]).ap()
    mem = din("mem", [256, D]).ap()
    ck = din("ck", [16, 256, 1024]).ap()
    cvv = din("cvv", [16, 256, 1024]).ap()
    sconv = din("sconv", [16, 30, 1024]).ap()
    sffn = din("sffn", [32, DFF]).ap()
    flag_d = din("flag", [128, 1]).ap()
    colsrc = din("colsrc", [CV_ROWS, 128]).ap()
    lnvg_h = din("lnvg", [1, DA])
    lnvb_h = din("lnvb", [1, DA])
    gfin_h = din("gfin", [1, D])
    bs_h = din("bs", [1, 8 * 128])
    ws_d = din("ws", [8, 128, 128]).ap()
    idb_d = din("idb", [128, 128], BF16).ap()
    idf_d = din("idf", [128, 128]).ap()
    tril_d = din("tril", [128, 128]).ap()
    w_in = din("w_in", [D, 11264]).ap()
    w_pa = din("w_pa", [DA, D]).ap()
    w_pb = din("w_pb", [DA, D]).ap()
    w_pc = din("w_pc", [DA, D]).ap()
    w_o = din("w_o", [D, D]).ap()
    w_k = din("w_k", [D, 1024]).ap()
    w_v = din("w_v", [D, 1024]).ap()
    w_up = din("w_up", [D, 2 * DFF]).ap()
    w_down = din("w_down", [DFF, D]).ap()

    yp = dout("yp", [1024, D])
    ys = dout("ys", [64, D])
    mk_o = dout("mk", [256, 1024])
    mv_o = dout("mv", [256, 1024])
    ncp_o = dout("ncp", [30, 1024])
    nfp_o = dout("nfp", [2, DFF])
    ncs_o = dout("ncs", [16, 30, 1024])
    nfs_o = dout("nfs", [32, DFF])
    vs_o = dout("vs", [64, DA])
    out_keys = []

    TMAX = 448
    with ExitStack() as st:
        P = Prog(nc, st)

        def sb(n, s, d=F32):
            return st.enter_context(nc.sbuf_tensor(n, list(s), d))

        X = sb("X", [128, 4, D])
        hT = sb("hT", [128, 16, TMAX], BF16)
        AR_KB = 68
        arena = sb("arena", [128, AR_KB * 512], BF16)
        NW = 7
        wslots = [sb(f"wsl{i}", [128, 16, 256], BF16) for i in range(NW)]
        P.lookahead = NW - 1
        bsb = sb("bsb", [128, 8, 128])
        WsT = sb("WsT", [128, 8, 128], BF16)
        WsTs = sb("WsTs", [64, 8, 64], BF16)
        colv = sb("colv", [128, CV_ROWS])
        idb = sb("idb_s", [128, 128], BF16)
        idf = sb("idf_s", [128, 128])
        tril = sb("tril_s", [128, 128])
        ones_f = sb("ones_f", [128, 128])
        ones_b = sb("ones_b", [128, 128], BF16)
        flag = sb("flag_s", [128, 1])
        KT = sb("KT", [128, 8, 256], BF16)
        Vb = sb("Vb", [128, 2, 1024], BF16)
        xs = sb("xs", [128, D], BF16)
        NTMP = 6
        tmps = [sb(f"tmp{i}", [128, 452]) for i in range(NTMP)]
        carry_g = sb("carry_g", [128, 8, 30])
        carry_f = sb("carry_f", [128, 2, NJ])
        small = sb("small", [128, 128])
        macc = [xs[:, i * 896:(i + 1) * 896].bitcast(F32) for i in range(2)]
        ptmp = []
        NDG = 4
        dgs = [sb(f"dg{i}", [128, 128]) for i in range(NDG)]
        ps = [st.enter_context(nc.psum_tensor(f"ps{i}", [128, 512], F32)) for i in range(8)]
        psb = [p[:].bitcast(BF16) for p in ps]

        def ar(kb0, nbytes, dt, pattern=None, **kw):
            e0 = int(kb0 * 512)
            v = arena[:, e0:e0 + nbytes // 2]
            if dt in (F32, F32R):
                v = v.bitcast(dt)
            if pattern:
                v = v.rearrange(pattern, **kw)
            return v

        uT = ar(0, 8 * TMAX * 2, BF16, "p (k t) -> p k t", k=8)
        qT = ar(7, 8 * TMAX * 2, BF16, "p (k t) -> p k t", k=8)
        bT = ar(14, 8 * TMAX * 2, BF16, "p (k t) -> p k t", k=8)
        R0 = 21
        vtok = ar(R0, 4 * DA * 4, F32, "p (s d) -> p s d", s=4)
        vn = ar(R0 + 16, 4 * DA * 2, BF16, "p (s d) -> p s d", s=4)
        vsf = ar(R0 + 24, DA * 4, F32)
        GW = 30 + 384
        gT = ar(R0, 8 * GW * 4, F32, "p (c t) -> p c t", c=8)
        gTs = ar(R0 + 13, 8 * 16 * 34 * 4, F32, "p (c q t) -> p c q t", c=8, q=16)
        gTr = ar(R0, 8 * GW * 4, F32R, "p (c t) -> p c t", c=8)
        gTsr = ar(R0 + 13, 8 * 16 * 34 * 4, F32R, "p (c q t) -> p c q t", c=8, q=16)
        cv = ar(R0 + 30, 8 * TMAX * 4, F32, "p (c t) -> p c t", c=8)
        sc = ar(R0 + 30, 4 * 1024 * 4, F32, "p (j c) -> p j c", j=4)
        Pm = [ar(R0 + 2 * i, 4 * 256 * 2, BF16, "p (h m) -> p h m", h=4) for i in range(2)]
        PT = [ar(R0 + 4 + 2 * i, 8 * 128 * 2, BF16, "p (k t) -> p k t", k=8) for i in range(2)]
        Kseq = [ar(R0 + 8 + 4 * i, 2 * 1024 * 2, BF16, "p (m e) -> p m e", m=2) for i in range(2)]
        Vseq = [ar(R0 + 16 + 4 * i, 2 * 1024 * 2, BF16, "p (m e) -> p m e", m=2) for i in range(2)]
        KTs = [ar(R0 + 24 + 4 * i, 8 * 256 * 2, BF16, "p (k m) -> p k m", k=8) for i in range(2)]
        expT = ar(R0 + 32, 2 * 256 * 2, BF16, "p (m c) -> p m c", m=2)
        cT = ar(61, 8 * TMAX * 2, BF16, "p (k t) -> p k t", k=8)
        mixT = ar(R0, 16 * TMAX * 2, BF16, "p (k t) -> p k t", k=16)
        yT = ar(0, NJ * TMAX * 2, BF16, "p (k t) -> p k t", k=NJ)
        hffn = ar(44, NJ * 32 * 4, F32, "p (j q t) -> p j q t", j=NJ, q=16)
        faSn = ar(50, NJ * 32 * 4, F32, "p (j r) -> p j r", j=NJ)
        stg = ar(0, 44 * 1024, F32)
        sst = ar(56, 8 * 1024, F32)
        gst = ar(R0 + 30, 4 * 1024, F32)
        wsn = ar(21, 8 * 128 * 4, F32, "p (g s) -> p g s", g=8)
        wsns = ar(25, 8 * 64 * 4, F32, "p (g s) -> p g s", g=8)
        cst = ar(27, 4 * 128 * 4, F32, "p (j c) -> p j c", j=4)
        memx = ar(29, 2 * D * 4, F32, "p (m d) -> p m d", m=2)
        kst = ar(45, 2 * 1024 * 4, F32, "p (m e) -> p m e", m=2)
        mT = ar(53, 16 * 256 * 2, BF16, "p (k m) -> p k m", k=16)

        lnvg_b = ar(49, DA * 4, F32)
        lnvb_b = ar(53, DA * 4, F32)
        gfin_b = ar(56, D * 4, F32)
        PH = ("ph",)

        def RD(reads, *aps):
            r = list(reads)
            if PH not in r and any(getattr(a, "name", None) == "arena" for a in aps):
                r.append(PH)
            return r

        def ACT(out, in_, func, reads, writes, **kw):
            P.op("act", lambda e: e.activation(out=out, in_=in_, func=func, **kw), RD(reads, out, in_), writes)

        def MM(out, lhsT, rhs, start, stop, reads, writes):
            P.op("pe", lambda e: e.matmul(out, lhsT=lhsT, rhs=rhs, start=start, stop=stop), RD(reads, lhsT, rhs), writes)

        def TR(out, in_, ident, reads, writes):
            P.op("pe", lambda e: e.transpose(out=out, in_=in_, identity=ident), RD(reads, in_), writes)

        def TT(out, in0, in1, op, reads, writes, eng="dve"):
            P.op(eng, lambda e: e.tensor_tensor(out=out, in0=in0, in1=in1, op=op), RD(reads, out, in0, in1), writes)

        def TS(out, in0, s1, s2, op0, op1, reads, writes, eng="dve"):
            if op1 is None:
                P.op(eng, lambda e: e.tensor_scalar(out=out, in0=in0, scalar1=s1, scalar2=None, op0=op0),
                     RD(reads, out, in0), writes)
            else:
                P.op(eng, lambda e: e.tensor_scalar(out=out, in0=in0, scalar1=s1, scalar2=s2, op0=op0, op1=op1),
                     RD(reads, out, in0), writes)

        def STT(out, in0, scalar, in1, op0, op1, reads, writes, eng="dve"):
            P.op(eng, lambda e: e.scalar_tensor_tensor(out=out, in0=in0, scalar=scalar, in1=in1, op0=op0, op1=op1),
                 RD(reads, out, in0, in1), writes)

        def CP(out, in_, reads, writes, eng="dve"):
            if eng == "act":
                P.op("act", lambda e: e.activation(out=out, in_=in_, func=AF.Copy), RD(reads, out, in_), writes)
            else:
                P.op(eng, lambda e: e.tensor_copy(out=out, in_=in_), RD(reads, out, in_), writes)

        def MSET(ap, val, writes, eng="dve"):
            P.op(eng, lambda e: e.memset(ap, val), RD((), ap), writes)

        def RCP(out, in_, reads, writes):
            P.op("dve", lambda e: e.reciprocal(out=out, in_=in_), RD(reads, out, in_), writes)

        def DMA(eng, out, in_, reads, writes, key):
            P.op(eng, lambda e: e.dma_start(out=out, in_=in_), RD(reads, out, in_), writes, dma=key)

        def barrier():
            P.op("sp", lambda e: e.nop(), (), [PH])

        tmp_i = [0]

        def T():
            i = tmp_i[0] % NTMP
            tmp_i[0] += 1
            return tmps[i], ("tmp", i)

        lin_i = [0]

        def LB():
            b = lin_i[0] % 4
            lin_i[0] += 1
            return b

        aux_i = [0]

        def AB():
            b = 4 + aux_i[0] % 4
            aux_i[0] += 1
            return b

        small_i = [0]

        def SM(n):
            a = small_i[0]
            small_i[0] += n
            assert small_i[0] <= 128
            return small[:, a:a + n], ("small", a)

        wi = [0]

        def witem(W, r0, nk, c0, ncol):
            s = wi[0] % NW
            wi[0] += 1
            src = W[r0 * 128:(r0 + nk) * 128, c0:c0 + ncol].rearrange("(k p) c -> p k c", p=128)
            dst = wslots[s][:, 0:nk, 0:ncol]
            P.wdma("pool", lambda e: e.dma_start(out=dst, in_=src), [("w", s)], f"w{s}")
            return s

        def bc_dram(h, n):
            return bass.AP(h, 0, [[0, 128], [1, n]])

        for s_, t_ in enumerate(passes[0][0]):
            DMA("sp", X[:, s_, :], xp[t_ * 128:(t_ + 1) * 128, :], (), [("X", s_)], f"X{s_}")
        if passes[0][1]:
            DMA("sp", X[0:64, len(passes[0][0]), :], xsm, (), [("X", len(passes[0][0]))], f"X{len(passes[0][0])}")
        if any(hs for _, hs in passes):
            DMA("sp", ncs_o[:, 0:26, :], sconv[:, 4:30, :], (), ["oncs0"], "oncs0")
        DMA("sp", idb[:], idb_d, (), ["idb"], "const")
        DMA("sp", idf[:], idf_d, (), ["idf"], "const")
        DMA("sp", tril[:], tril_d, (), ["tril"], "const")
        DMA("sp", flag[:], flag_d, (), ["flag"], "const")
        if "bc" not in SKIP:
            DMA("sp", bsb[:].rearrange("p g t -> p (g t)"), bc_dram(bs_h, 1024), (), ["bsb"], "const")
        for j in range(4):
            DMA("sp", cst[:, j, :], colsrc[j * 128:(j + 1) * 128, :], [PH], [("cst", j)], "const")
        DMA("sp", wsn, ws_d.rearrange("g t s -> t g s"), [PH], ["wsn"], "const")
        for i_ in range(NTMP):
            MSET(tmps[i_][:], 0.0, [("tmp", i_)])
        MSET(ones_f[:], 1.0, ["ones_f"])
        MSET(ones_b[:], 1.0, ["ones_b"])
        MSET(carry_g[:], 0.0, ["carry_g"])
        MSET(carry_f[:], 0.0, ["carry_f"])
        wsq = [("wsns", q) for q in range(16)]
        MSET(wsns[0:64], 0.0, wsq)
        for q in range(16 if "wsnsdma" not in SKIP else 0):
            DMA("sp", wsns[4 * q:4 * q + 4, :, 4 * q:4 * q + 4], ws_d[:, 0:4, 0:4].rearrange("g t s -> t g s"),
                [PH], [("wsns", q)], "const2")
        b = AB()
        for j in range(4):
            TR(ps[b][:, j * 128:(j + 1) * 128], cst[:, j, :], idf[:], [("cst", j), "idf", PH], [("ps", b)])
        CP(colv[:], ps[b][:], [("ps", b)], ["colv"])
        TT(wsn, wsn, tril[:].unsqueeze(1).to_broadcast([128, 8, 128]), ALU.mult, ["wsn", "tril", PH], ["wsn"])
        for hh in range(2):
            b = AB()
            for g in range(4):
                TR(ps[b][:, g * 128:(g + 1) * 128], wsn[:, hh * 4 + g, :], idf[:], ["wsn", "idf", PH], [("ps", b)])
            CP(WsT[:, hh * 4:hh * 4 + 4, :], ps[b][:].rearrange("p (g t) -> p g t", g=4), [("ps", b)], ["WsT"])
        TT(wsns[0:64], wsns[0:64], tril[0:64, 0:64].unsqueeze(1).to_broadcast([64, 8, 64]), ALU.mult, wsq + ["tril", PH], ["wsns"])
        b = AB()
        for g in range(8 if "wsnstr" not in SKIP else 0):
            TR(ps[b][0:64, g * 64:(g + 1) * 64], wsns[0:64, g, :], idf[0:64, 0:64], ["wsns", "idf", PH], [("ps", b)])
        CP(WsTs[:], ps[b][0:64, :].rearrange("p (g t) -> p g t", g=8), [("ps", b)], ["WsTs"])

        def norm_T(srcs, gidx, dst, dst_res, tag):
            small_i[0] = 0
            ss, ss_r = SM(4)
            ms, ms_r = SM(4)
            rstd, rstd_r = SM(4)
            MSET(ss, 0.0, [ss_r])
            for i, (src, rows, c0, res) in enumerate(srcs):
                ACT(xs[:rows], src, AF.Square, [res, ss_r], ["xs", ss_r], accum_out=ss[:rows, i:i + 1])
            TS(ms, ss, 1.0 / D, EPS, ALU.mult, ALU.add, [ss_r], [ms_r])
            ACT(ms, ms, AF.Sqrt, [ms_r], [ms_r])
            RCP(rstd, ms, [ms_r], [rstd_r])
            for i, (src, rows, c0, res) in enumerate(srcs):
                ACT(xs[:rows], src, AF.Copy, [res, rstd_r], ["xs"], scale=rstd[:rows, i:i + 1])
                for hh in range(2):
                    b = AB()
                    pv = psb[b].rearrange("p (k t) -> p k t", k=8)
                    for k in range(8):
                        kk = hh * 8 + k
                        TR(pv[:, k, 0:rows], xs[:rows, kk * 128:(kk + 1) * 128], idb[:rows, :rows],
                           ["xs", "idb"], [("ps", b)])
                    TT(dst[:, hh * 8:hh * 8 + 8, c0:c0 + rows], pv[:, :, 0:rows],
                       colv[:, gidx + hh * 8:gidx + hh * 8 + 8].unsqueeze(2).to_broadcast([128, 8, rows]),
                       ALU.mult, [("ps", b), "colv"], [dst_res(i)])

        def lin_item(W, r0, c0, ncol, K, act_fn, act_reads, ncols, banks=None, first=True, last=True, cs=0):
            s = witem(W, r0, K, c0, ncol)
            nh = ncol // 128
            if banks is None:
                banks = [LB() for _ in range(nh)]
            for h in range(nh):
                for k in range(K):
                    MM(ps[banks[h]][:, cs:ncols], wslots[s][:, k, h * 128:(h + 1) * 128], act_fn(r0 + k),
                       first and k == 0, last and k == K - 1, [("w", s)] + act_reads, [("ps", banks[h])])
            return banks

        for m in range(2 if "kv" not in SKIP else 0):
            DMA("sp", memx[:, m, :], mem[m * 128:(m + 1) * 128, :], [PH], [("memx", m)], f"memx{m}")
        if "kv" not in SKIP:
            norm_T([(memx[:, m, :], 128, m * 128, ("memx", m)) for m in range(2)], CV_GMEM, mT, lambda i: "mT", "m")
        for which, W, okey, o_ap in ((("k", w_k, "omk", mk_o), ("v", w_v, "omv", mv_o)) if ("kv" not in SKIP and "kvlin" not in SKIP) else ()):
            for i in range(4):
                banks = lin_item(W, 0, i * 256, 256, 16, lambda k: mT[:, k, :], ["mT", PH], 256)
                for h in range(2):
                    ec = 2 * i + h
                    t, tr = T()
                    CP(t[:, 0:256], ps[banks[h]][:, 0:256], [("ps", banks[h])], [tr], eng="act")
                    if which == "k":
                        CP(KT[:, ec, :], t[:, 0:256], [tr], ["KT"])
                    if "kvtr" in SKIP:
                        continue
                    b = AB()
                    for m in range(2):
                        TR(ps[b][:, m * 128:(m + 1) * 128], t[:, m * 128:(m + 1) * 128], idf[:], [tr, "idf"], [("ps", b)])
                    CP(kst[:, :, ec * 128:(ec + 1) * 128], ps[b][:, 0:256].rearrange("p (m e) -> p m e", m=2),
                       [("ps", b), PH], ["kst"], eng="act")
            if which == "v" and "kvtr" not in SKIP:
                CP(Vb[:].rearrange("p m e -> p (m e)"), kst.rearrange("p m e -> p (m e)"), ["kst", PH], ["Vb"])
            if "kvout" not in SKIP:
                DMA("sp", o_ap.rearrange("(m p) e -> p m e", p=128), kst, ["kst", PH], [okey], okey)
            out_keys.append(okey)
        barrier()

        for pi, (ptiles, has_s) in enumerate(passes):
            ntp = len(ptiles)
            ncp = ntp * 128
            ncols = ncp + (64 if has_s else 0)
            nsl = ntp + (1 if has_s else 0)
            slots = []
            for s in range(ntp):
                slots.append((s, 128, s * 128))
            if has_s:
                slots.append((ntp, 64, ncp))
            hT_reads = [("hT", s) for s in range(nsl)]
            first_pass = ptiles[0] == 0

            if stop == "setup":
                break
            if pi > 0:
                for s, t in enumerate(ptiles):
                    DMA("sp", X[:, s, :], xp[t * 128:(t + 1) * 128, :], (), [("X", s)], f"X{s}")
                if has_s:
                    DMA("sp", X[0:64, ntp, :], xsm, (), [("X", ntp)], f"X{ntp}")
            norm_T([(X[:rows, s, :], rows, c0, ("X", s)) for (s, rows, c0) in slots], CV_GMIX, hT,
                   lambda i: ("hT", i), "n1")
            cs = 96 if first_pass else 0
            hfn = lambda k: hT[:, k, cs:ncols]
            hfn_full = lambda k: hT[:, k, 0:ncols]

            if stop == "S1" and pi == len(passes) - 1:
                break
            DMA("sp", lnvg_b, bc_dram(lnvg_h, DA), [PH], ["lnvg"], "lnvg")
            DMA("sp", lnvb_b, bc_dram(lnvb_h, DA), [PH], ["lnvb"], "lnvb")
            for i in range(4):
                banks = lin_item(w_in, 0, 1024 + i * 256, 256, 16, hfn_full, hT_reads, ncols)
                for h in range(2):
                    d = 2 * i + h
                    t, tr = T()
                    ACT(t[:, 0:ncols], ps[banks[h]][:, 0:ncols], AF.Gelu_apprx_tanh, [("ps", banks[h])], [tr])
                    b = AB()
                    for (s, rows, c0) in slots:
                        TR(ps[b][:rows, s * 128:(s + 1) * 128], t[:, c0:c0 + rows], idf[:], [tr, "idf"], [("ps", b)])
                    CP(vtok[:, 0:ntp, d * 128:(d + 1) * 128], ps[b][:, 0:ncp].rearrange("p (s e) -> p s e", s=ntp),
                       [("ps", b), PH], [("vtok", d)])
                    if has_s:
                        CP(vtok[0:64, ntp, d * 128:(d + 1) * 128], ps[b][0:64, ncp:ncp + 128],
                           [("ps", b), PH], [("vtok", d)])
            for i in range(4):
                banks = lin_item(w_in, 0, i * 256, 256, 16, hfn, hT_reads, ncols, cs=cs)
                for h in range(2):
                    d = 2 * i + h
                    ACT(uT[:, d, cs:ncols], ps[banks[h]][:, cs:ncols], AF.Gelu_apprx_tanh, [("ps", banks[h]), PH], [("uT", d)])
            vt_reads = [("vtok", d) for d in range(8)] + [PH]
            small_i[0] = 16
            st6, st6_r = SM(nsl * 12)
            mvv, mv_r = SM(nsl * 2)
            rs_, rs_r = SM(nsl)
            MSET(mvv, 1.0, [mv_r])
            for (s, rows, c0) in slots:
                for hh in range(2):
                    P.op("dve", lambda e, o_=st6[:rows, s * 12 + hh * 6:s * 12 + hh * 6 + 6],
                         i_=vtok[:rows, s, hh * 512:(hh + 1) * 512]: e.bn_stats(out=o_, in_=i_), vt_reads, [st6_r])
                P.op("dve", lambda e, o_=mvv[:rows, 2 * s:2 * s + 2], i_=st6[:rows, s * 12:s * 12 + 12]: e.bn_aggr(out=o_, in_=i_),
                     [st6_r], [mv_r])
            mvw = mvv.rearrange("p (s two) -> p s two", two=2)
            TS(rs_, mvw[:, :, 1], EPS, None, ALU.add, None, [mv_r], [rs_r])
            ACT(rs_, rs_, AF.Sqrt, [rs_r], [rs_r])
            RCP(rs_, rs_, [rs_r], [rs_r])
            for (s, rows, c0) in slots:
                TS(vtok[:rows, s, :], vtok[:rows, s, :], mvv[:rows, 2 * s:2 * s + 1], rs_[:rows, s:s + 1],
                   ALU.subtract, ALU.mult, vt_reads + [mv_r, rs_r], [("vtk2", s)])
                TT(vtok[:rows, s, :], vtok[:rows, s, :], lnvg_b[:rows], ALU.mult, [("vtk2", s), "lnvg", PH], [("vtk2", s)])
                if rows == 128:
                    TT(vn[:, s, :], vtok[:, s, :], lnvb_b, ALU.add, [("vtk2", s), "lnvb", PH], [("vn", s)])
                else:
                    TT(vsf[:rows], vtok[:rows, s, :], lnvb_b[:rows], ALU.add, [("vtk2", s), "lnvb", PH], ["vsf"])
                    CP(vn[:rows, s, :], vsf[:rows], ["vsf", PH], [("vn", s)], eng="act")
                    DMA("sp", vs_o, vsf[0:64], ["vsf", PH], ["ovs"], "ovs")
                    out_keys.append("ovs")
            for (s, rows, c0) in slots:
                tl = cs if (first_pass and s == 0) else 0
                for gh in range(2):
                    b = AB()
                    pv = ps[b][:].rearrange("p (g t) -> p g t", g=4)
                    for g4 in range(4):
                        g = gh * 4 + g4
                        if rows == 128:
                            MM(pv[:, g4, tl:128], vn[:, s, g * 128:(g + 1) * 128], WsT[:, g, tl:128], True, True,
                               [("vn", s), "WsT", PH], [("ps", b)])
                        else:
                            MM(pv[:, g4, 0:64], vn[0:64, s, g * 128:(g + 1) * 128], WsTs[:, g, :], True, True,
                               [("vn", s), "WsTs", PH], [("ps", b)])
                    ur = [("uT", gh * 4 + g4) for g4 in range(4)]
                    if rows == 128:
                        for g2 in range(2):
                            tw, twr = T()
                            twv = tw[:, 0:256].rearrange("p (g t) -> p g t", g=2)
                            ga = gh * 4 + g2 * 2
                            TT(twv[:, :, tl:128], pv[:, g2 * 2:g2 * 2 + 2, tl:128], bsb[:, ga:ga + 2, tl:128], ALU.add, [("ps", b), "bsb"], [twr])
                            TT(uT[:, ga:ga + 2, c0 + tl:c0 + 128], twv[:, :, tl:128], uT[:, ga:ga + 2, c0 + tl:c0 + 128], ALU.mult,
                               [twr, ("uT", ga), ("uT", ga + 1), PH], [("uT", ga), ("uT", ga + 1)])
                    else:
                        tw, twr = T()
                        twv = tw[:, 0:256].rearrange("p (g q t) -> p g q t", g=4, q=16)
                        ga = gh * 4
                        TT(twv, pv[:, :, 0:64].rearrange("p g (q t) -> p g q t", t=4),
                           bsb[:, ga:ga + 4, 0:4].unsqueeze(2).to_broadcast([128, 4, 16, 4]), ALU.add, [("ps", b), "bsb"], [twr])
                        TT(uT[:, ga:ga + 4, c0:c0 + 64], tw[:, 0:256].rearrange("p (g t) -> p g t", g=4),
                           uT[:, ga:ga + 4, c0:c0 + 64], ALU.mult, [twr, PH] + ur, ur)
            if debug and pi == 0:
                d1 = nc.dram_tensor("dbg_aT", [128, 8, 384], BF16, kind="ExternalOutput").ap()
                d2 = nc.dram_tensor("dbg_vn", [128, 3, DA], BF16, kind="ExternalOutput").ap()
                d3 = nc.dram_tensor("dbg_WsT", [128, 1024], BF16, kind="ExternalOutput").ap()
                DMA("sp", d1, uT[:, :, 0:384], [("uT", k) for k in range(8)] + [PH], ["dbg1"], "dbg1")
                DMA("sp", d2, vn[:, 0:3, :], [("vn", k) for k in range(3)] + [PH], ["dbg2"], "dbg2")
                DMA("sp", d3, WsT[:].rearrange("p g t -> p (g t)"), ["WsT"], ["dbg3"], "dbg3")
            barrier()
            if stop == "A" and pi == len(passes) - 1:
                break

            if first_pass:
                MSET(gT[:, :, 0:30 + cs], 0.0, [("gT", c) for c in range(8)] + [PH])
            else:
                CP(gT[:, :, 0:30], carry_g[:], ["carry_g", PH], [("gT", c) for c in range(8)])
            if has_s:
                for j in range(4):
                    DMA("sp", sc[0:120, j, :], sconv[4 * j:4 * j + 4].rearrange("q r c -> (q r) c"), [PH], [("sc", j)], f"sc{j}")
                for c in range(8):
                    b = AB()
                    for j in range(4):
                        TR(ps[b][:, j * 120:(j + 1) * 120], sc[0:120, j, c * 128:(c + 1) * 128], idf[0:120, 0:120],
                           [("sc", j), "idf", PH], [("ps", b)])
                    CP(gTs[:, c, :, 0:30], ps[b][:, 0:480].rearrange("p (q r) -> p q r", r=30), [("ps", b), PH], [("gTsh", c)],
                       eng="act" if c % 2 else "dve")
                P.op("sp", lambda e: e.nop(), [PH], [("sc", j) for j in range(4)] + [("phcv",)])
            for i in range(4):
                ba = lin_item(w_in, 0, 2048 + i * 256, 256, 16, hfn, hT_reads, ncols, cs=cs)
                bb = lin_item(w_in, 0, 3072 + i * 256, 256, 16, hfn, hT_reads, ncols, cs=cs)
                for h in range(2):
                    c = 2 * i + h
                    t, tr = T()
                    ACT(t[:, cs:ncols], ps[bb[h]][:, cs:ncols], AF.Sigmoid, [("ps", bb[h])], [tr])
                    TT(gT[:, c, 30 + cs:30 + ncp], ps[ba[h]][:, cs:ncp], t[:, cs:ncp], ALU.mult, [("ps", ba[h]), tr, PH], [("gT", c)])
                    if has_s:
                        TT(gTs[:, c, :, 30:34], ps[ba[h]][:, ncp:ncp + 64].rearrange("p (q t) -> p q t", t=4),
                           t[:, ncp:ncp + 64].rearrange("p (q t) -> p q t", t=4), ALU.mult, [("ps", ba[h]), tr, PH], [("gTs", c)])
                    if first_pass:
                        TS(gT[:, c, 126:158], gT[:, c, 126:158], flag[:, 0:1], None, ALU.mult, None, [("gT", c), "flag", PH], [("gT", c)])
            cvr = [("phcv",), PH]
            NPE = 11
            KD = 31 - NPE
            dgi = 0
            for c in range(8):
                ACT(cv[:, c, cs:ncp], gT[:, c, cs:ncp], AF.Identity, [("gT", c), "colv"] + cvr, [("cv", c)],
                    scale=colv[:, CV_CONVW + c:CV_CONVW + c + 1], bias=colv[:, CV_CONVB + c:CV_CONVB + c + 1])
                if has_s:
                    ACT(cv[:, c, ncp:ncp + 64].rearrange("p (q t) -> p q t", t=4), gTs[:, c, :, 0:4], AF.Identity,
                        [("gTs", c), ("gTsh", c), "colv"] + cvr, [("cvs", c)],
                        scale=colv[:, CV_CONVW + c:CV_CONVW + c + 1], bias=colv[:, CV_CONVB + c:CV_CONVB + c + 1])
            pebanks = []
            GC = 2 if has_s else 4
            for c in range(8):
                gi_ = c % GC
                b = 4 + (2 * gi_ if has_s else gi_)
                b2 = 5 + 2 * gi_ if has_s else None
                pebanks.append((b, b2))
                for k in range(KD, 31):
                    dg = dgs[dgi % NDG]
                    dgr = ("dg", dgi % NDG)
                    dgi += 1
                    wc = colv[:, CV_CONVW + k * 8 + c:CV_CONVW + k * 8 + c + 1]
                    ACT(dg[:], idf[:], AF.Copy, ["idf", "colv"], [dgr], scale=wc)
                    MM(ps[b][:, cs:ncp], dg[:], gT[:, c, cs + k:k + ncp], k == KD, k == 30,
                       [dgr, ("gT", c)] + cvr, [("ps", b)])
                    if has_s:
                        MM(ps[b2][:, 0:64].rearrange("p (q t) -> p q t", t=4), dg[:], gTs[:, c, :, k:k + 4],
                           k == KD, k == 30, [dgr, ("gTs", c), ("gTsh", c)] + cvr, [("ps", b2)])
                if c % GC == GC - 1:
                    for k in range(1, KD):
                        for cc in range(c - GC + 1, c + 1):
                            wc = colv[:, CV_CONVW + k * 8 + cc:CV_CONVW + k * 8 + cc + 1]
                            STT(cv[:, cc, cs:ncp], gT[:, cc, cs + k:k + ncp], wc, cv[:, cc, cs:ncp], ALU.mult, ALU.add,
                                [("gT", cc), ("cv", cc), "colv", PH], [("cv", cc)])
                            if has_s:
                                cvs = cv[:, cc, ncp:ncp + 64].rearrange("p (q t) -> p q t", t=4)
                                STT(cvs, gTs[:, cc, :, k:k + 4], wc, cvs, ALU.mult, ALU.add,
                                    [("gTs", cc), ("gTsh", cc), ("cvs", cc), "colv", PH], [("cvs", cc)])
                    for cc in range(c - GC + 1, c + 1):
                        bb, bb2 = pebanks[cc]
                        TT(cv[:, cc, cs:ncp], cv[:, cc, cs:ncp], ps[bb][:, cs:ncp], ALU.add, [("cv", cc), ("ps", bb), PH], [("cv", cc)])
                        if has_s:
                            TT(cv[:, cc, ncp:ncp + 64], cv[:, cc, ncp:ncp + 64], ps[bb2][:, 0:64], ALU.add,
                               [("cvs", cc), ("ps", bb2), PH], [("cvs", cc)])
            for i in range(4):
                banks = lin_item(w_in, 0, 4096 + i * 256, 256, 16, hfn, hT_reads, ncols, cs=cs)
                for h in range(2):
                    ec = 2 * i + h
                    CP(qT[:, ec, cs:ncols], ps[banks[h]][:, cs:ncols], [("ps", banks[h]), PH], [("qT", ec)], eng="act")
            CP(carry_g[:], gT[:, :, ncp:ncp + 30], [("gT", c) for c in range(8)] + [PH], ["carry_g"])
            b1, b2 = AB(), AB()
            for c in range(8):
                t, tr = T()
                ACT(t[:, cs:ncols], cv[:, c, cs:ncols], AF.Square, [("cv", c), ("cvs", c), PH], [tr])
                MM(ps[b1][:, cs:ncols], ones_f[:], cv[:, c, cs:ncols], c == 0, c == 7, [("cv", c), ("cvs", c), "ones_f", PH], [("ps", b1)])
                MM(ps[b2][:, cs:ncols], ones_f[:], t[:, cs:ncols], c == 0, c == 7, [tr, "ones_f"], [("ps", b2)])
            mean, mean_r = T()
            msq, msq_r = T()
            rsb, rsb_r = T()
            TS(mean[:, cs:ncols], ps[b1][:, cs:ncols], 1.0 / DA, None, ALU.mult, None, [("ps", b1)], [mean_r])
            TT(msq[:, cs:ncols], mean[:, cs:ncols], mean[:, cs:ncols], ALU.mult, [mean_r], [msq_r])
            STT(rsb[:, cs:ncols], ps[b2][:, cs:ncols], 1.0 / DA, msq[:, cs:ncols], ALU.mult, ALU.subtract, [("ps", b2), msq_r], [rsb_r])
            TS(rsb[:, cs:ncols], rsb[:, cs:ncols], EPS, None, ALU.add, None, [rsb_r], [rsb_r])
            ACT(rsb[:, cs:ncols], rsb[:, cs:ncols], AF.Sqrt, [rsb_r], [rsb_r])
            RCP(rsb[:, cs:ncols], rsb[:, cs:ncols], [rsb_r], [rsb_r])
            for c in range(8):
                TT(cv[:, c, cs:ncols], cv[:, c, cs:ncols], mean[:, cs:ncols], ALU.subtract, [("cv", c), ("cvs", c), mean_r, PH], [("cv", c)])
                TT(cv[:, c, cs:ncols], cv[:, c, cs:ncols], rsb[:, cs:ncols], ALU.mult, [("cv", c), rsb_r, PH], [("cv", c)])
                ACT(bT[:, c, cs:ncols], cv[:, c, cs:ncols], AF.Silu, [("cv", c), "colv", PH], [("bT", c)],
                    scale=colv[:, CV_LNBG + c:CV_LNBG + c + 1], bias=colv[:, CV_LNBB + c:CV_LNBB + c + 1])
            if has_s:
                cvall = [("cv", c) for c in range(8)]
                for half in range(2):
                    gn, gn_r = T()
                    CP(gn[:, 0:256].rearrange("p (c q t) -> p c q t", c=4, q=16), gTs[:, half * 4:half * 4 + 4, :, 30:34],
                       [("gTs", c) for c in range(8)] + [PH], [gn_r])
                    b = AB()
                    for c4 in range(4):
                        TR(ps[b][0:64, c4 * 128:(c4 + 1) * 128], gn[:, c4 * 64:(c4 + 1) * 64], idf[:], [gn_r, "idf"], [("ps", b)])
                    CP(gst[0:64, half * 512:(half + 1) * 512], ps[b][0:64, :], [("ps", b), PH], cvall)
                for q in range(16):
                    DMA("sp", ncs_o[q, 26:30, :], gst[4 * q:4 * q + 4, :], cvall + [PH], ["oncs1"], "oncs1")
                out_keys.append("oncs1")
            barrier()
            if stop == "B" and pi == len(passes) - 1:
                break

            q_reads = [("qT", ec) for ec in range(8)] + [PH]
            for (s, rows, c0) in slots:
                if rows != 128:
                    continue
                r0 = cs if (first_pass and s == 0) else 0
                nr = 128 - r0
                q0 = c0 + r0
                bs2 = [AB(), AB()]
                for h in range(4):
                    pvw = ps[bs2[h // 2]][:].rearrange("p (h m) -> p h m", h=2)
                    for e2 in range(2):
                        ec = 2 * h + e2
                        MM(pvw[0:nr, h % 2, :], qT[:, ec, q0:q0 + nr], KT[:, ec, :], e2 == 0, e2 == 1,
                           q_reads + ["KT"], [("ps", bs2[h // 2])])
                small_i[0] = 80
                mx, mx_r = SM(4)
                se, se_r = SM(4)
                for hh in range(2):
                    P.op("dve", lambda e, o_=mx[0:nr, hh * 2:hh * 2 + 2], i_=ps[bs2[hh]][0:nr, :].rearrange("p (h m) -> p h m", h=2):
                         e.reduce_max(out=o_, in_=i_, axis=AX.X), [("ps", bs2[hh])], [mx_r])
                TS(mx[0:nr], mx[0:nr], -1.0 / 16, None, ALU.mult, None, [mx_r], [mx_r])
                MSET(se, 0.0, [se_r])
                pm = Pm[s % 2]
                pmr = ("Pm", s % 2)
                for h in range(4):
                    pvw = ps[bs2[h // 2]][:].rearrange("p (h m) -> p h m", h=2)
                    ACT(pm[0:nr, h, :], pvw[0:nr, h % 2, :], AF.Exp, [("ps", bs2[h // 2]), mx_r, se_r, PH], [pmr, se_r],
                        scale=1.0 / 16, bias=mx[0:nr, h:h + 1], accum_out=se[0:nr, h:h + 1])
                RCP(se[0:nr], se[0:nr], [se_r], [se_r])
                TT(pm[0:nr], pm[0:nr], se[0:nr].unsqueeze(2).to_broadcast([nr, 4, 256]), ALU.mult, [pmr, se_r, PH], [pmr])
                b = AB()
                pvb = psb[b].rearrange("p (k t) -> p k t", k=8)
                for h in range(4):
                    for m in range(2):
                        TR(pvb[:, h * 2 + m, 0:nr], pm[0:nr, h, m * 128:(m + 1) * 128], idb[0:nr, 0:nr], [pmr, "idb", PH], [("ps", b)])
                pt = PT[s % 2]
                ptr = ("PT", s % 2)
                CP(pt[:, :, 0:nr], pvb[:, :, 0:nr], [("ps", b), PH], [ptr], eng="act")
                for eh in range(2):
                    b = AB()
                    pv4 = ps[b][:].rearrange("p (k t) -> p k t", k=4)
                    for e4 in range(4):
                        ec = eh * 4 + e4
                        h = ec // 2
                        for m in range(2):
                            MM(pv4[:, e4, 0:nr], Vb[:, m, ec * 128:(ec + 1) * 128], pt[:, h * 2 + m, 0:nr], m == 0, m == 1,
                               [ptr, "Vb", PH], [("ps", b)])
                    CP(cT[:, eh * 4:eh * 4 + 4, q0:q0 + nr], pv4[:, :, 0:nr], [("ps", b), PH], [("cT", eh * 4 + e) for e in range(4)],
                       eng="dve" if eh else "act")
            if has_s:
                c0s = ncp
                bpv = 0
                pvs = ps[bpv][:].rearrange("p (k t) -> p k t", k=8)
                def st_KD(q):
                    bf = q % 2
                    DMA("pool", Kseq[bf], ck[q].rearrange("(m p) e -> p m e", p=128), [PH], [("Kseq", bf)], f"ks{bf}")

                def st_TK(q):
                    bf = q % 2
                    for eh in range(2):
                        b = AB()
                        pvb = psb[b].rearrange("p (k m) -> p k m", k=4)
                        for e4 in range(4):
                            ec = eh * 4 + e4
                            for m in range(2):
                                TR(pvb[:, e4, m * 128:(m + 1) * 128], Kseq[bf][:, m, ec * 128:(ec + 1) * 128], idb[:],
                                   [("Kseq", bf), "idb", PH], [("ps", b)])
                        CP(KTs[bf][:, eh * 4:eh * 4 + 4, :], pvb, [("ps", b), PH], [("KTs", bf)], eng="act" if eh else "dve")
                    if q + 2 < 16:
                        st_KD(q + 2)

                def st_VD(q):
                    bf = q % 2
                    DMA("pool", Vseq[bf], cvv[q].rearrange("(m p) e -> p m e", p=128), [PH], [("Vseq", bf)], f"vq{bf}")

                def st_S(q):
                    bf = q % 2
                    b = AB()
                    psc = ps[b][:, 0:32].rearrange("p (m h t) -> p m h t", m=2, h=4)
                    for m in range(2):
                        for h in range(4):
                            for e2 in range(2):
                                ec = 2 * h + e2
                                MM(psc[:, m, h, :], KTs[bf][:, ec, m * 128:(m + 1) * 128], qT[:, ec, c0s + 4 * q:c0s + 4 * q + 4],
                                   e2 == 0, e2 == 1, [("KTs", bf), PH] + q_reads, [("ps", b)])
                    ACT(expT[:, :, q * 16:(q + 1) * 16], ps[b][:, 0:32].rearrange("p (m c) -> p m c", m=2), AF.Exp,
                        [("ps", b), PH], [("expT", q)], scale=1.0 / 16)

                def st_V(q):
                    bf = q % 2
                    for ec in range(8):
                        h = ec // 2
                        for m in range(2):
                            MM(pvs[:, ec, 4 * q:4 * q + 4], Vseq[bf][:, m, ec * 128:(ec + 1) * 128],
                               expT[:, m, q * 16 + h * 4:q * 16 + h * 4 + 4], m == 0, m == 1,
                               [("Vseq", bf), ("expT", q), PH], [("ps", bpv)])

                st_KD(0); st_VD(0); st_KD(1); st_VD(1)
                st_TK(0); st_TK(1)
                for q in range(16):
                    st_S(q)
                    if q + 2 < 16:
                        st_TK(q + 2)
                    st_V(q)
                    if q + 2 < 16:
                        st_VD(q + 2)
                b = AB()
                for m in range(2):
                    MM(ps[b][:, 0:256], ones_b[:], expT[:, m, :], m == 0, m == 1, [("expT", q) for q in range(16)] + ["ones_b", PH], [("ps", b)])
                rsum, rsum_r = T()
                RCP(rsum[:, 0:256], ps[b][:, 0:256], [("ps", b)], [rsum_r])
                rsv = rsum[:, 0:256].rearrange("p (q h t) -> p q h t", q=16, h=4)
                for ec in range(8):
                    TT(cT[:, ec, c0s:c0s + 64].rearrange("p (q t) -> p q t", t=4), pvs[:, ec, :].rearrange("p (q t) -> p q t", t=4),
                       rsv[:, :, ec // 2, :], ALU.mult, [("ps", bpv), rsum_r, PH], [("cT", ec)])
            barrier()
            if stop == "C" and pi == len(passes) - 1:
                break

            for i in range(8):
                for bi, (Wp, actT, aname) in enumerate(((w_pa, uT, "uT"), (w_pb, bT, "bT"), (w_pc, cT, "cT"))):
                    bo = lin_item(Wp, 0, i * 256, 256, 8, lambda k, a=actT: a[:, k, cs:ncols],
                                  [(aname, k) for k in range(8)] + [PH], ncols, cs=cs)
                    bg = lin_item(w_in, 0, 5120 + bi * 2048 + i * 256, 256, 16, hfn, hT_reads, ncols, cs=cs)
                    for h in range(2):
                        t, tr = T()
                        ACT(t[:, cs:ncols], ps[bg[h]][:, cs:ncols], AF.Sigmoid, [("ps", bg[h])], [tr])
                        mm, mm_r = macc[h], ("macc", h)
                        if bi == 0:
                            TT(mm[:, cs:ncols], ps[bo[h]][:, cs:ncols], t[:, cs:ncols], ALU.mult, [("ps", bo[h]), tr], [mm_r])
                        elif bi == 1:
                            TT(t[:, cs:ncols], ps[bo[h]][:, cs:ncols], t[:, cs:ncols], ALU.mult, [("ps", bo[h]), tr], [tr])
                            TT(mm[:, cs:ncols], mm[:, cs:ncols], t[:, cs:ncols], ALU.add, [mm_r, tr], [mm_r])
                        else:
                            TT(t[:, cs:ncols], ps[bo[h]][:, cs:ncols], t[:, cs:ncols], ALU.mult, [("ps", bo[h]), tr], [tr])
                            TT(mixT[:, 2 * i + h, cs:ncols], mm[:, cs:ncols], t[:, cs:ncols], ALU.add, [mm_r, tr, PH],
                               [("mixT", 2 * i + h)])

            if stop == "S3" and pi == len(passes) - 1:
                break
            def add_to_X(bank, fo):
                t, tr = T()
                CP(t[:, cs:ncols], ps[bank][:, cs:ncols], [("ps", bank)], [tr], eng="act")
                b = AB()
                for (s, rows, c0) in slots:
                    TR(ps[b][:rows, s * 128:(s + 1) * 128], t[:, c0:c0 + rows], idf[:], [tr, "idf"], [("ps", b)])
                xr = [("X", s) for s in range(ntp)]
                TT(X[:, 0:ntp, fo * 128:(fo + 1) * 128], X[:, 0:ntp, fo * 128:(fo + 1) * 128],
                   ps[b][:, 0:ncp].rearrange("p (s e) -> p s e", s=ntp), ALU.add, [("ps", b)] + xr, xr)
                if has_s:
                    TT(X[0:64, ntp, fo * 128:(fo + 1) * 128], X[0:64, ntp, fo * 128:(fo + 1) * 128],
                       ps[b][0:64, ncp:ncp + 128], ALU.add, [("ps", b), ("X", ntp)], [("X", ntp)])

            mix_reads = [("mixT", k) for k in range(16)] + [PH]
            for i in range(8):
                banks = lin_item(w_o, 0, i * 256, 256, 16, lambda k: mixT[:, k, cs:ncols], mix_reads, ncols, cs=cs)
                for h in range(2):
                    add_to_X(banks[h], 2 * i + h)
            barrier()

            if stop == "S4" and pi == len(passes) - 1:
                break
            norm_T([(X[:rows, s, :], rows, c0, ("X", s)) for (s, rows, c0) in slots], CV_GFFN, hT,
                   lambda i: ("hT", i), "n2")

            if has_s:
                for j in range(NJ):
                    if j % 16 == 0:
                        jn = min(16, NJ - j)
                        DMA("sp", sst[0:32, 0:jn * 128], sffn[:, j * 128:(j + jn) * 128], [PH], ["sfst"], "sfst")
                    if j % 4 == 0:
                        b = AB()
                    TR(ps[b][:, (j % 4) * 32:(j % 4) * 32 + 32], sst[0:32, (j % 16) * 128:(j % 16 + 1) * 128], idf[0:32, 0:32],
                       ["sfst", "idf", PH], [("ps", b)])
                    if j % 4 == 3 or j == NJ - 1:
                        n = j % 4 + 1
                        j0 = j - (j % 4)
                        CP(hffn[:, j0:j0 + n, :, :], ps[b][:, 0:n * 32].rearrange("p (j q t) -> p j q t", j=n, q=16),
                           [("ps", b), PH], ["hffn"])
            for i in range(22):
                ncw = 256 if i < 21 else 128
                ba = lin_item(w_up, 0, i * 256, ncw, 16, hfn, hT_reads, ncols, cs=cs)
                bg = lin_item(w_up, 0, DFF + i * 256, ncw, 16, hfn, hT_reads, ncols, cs=cs)
                for h in range(ncw // 128):
                    j = 2 * i + h
                    fa, fa_r = T()
                    fc, fc_r = T()
                    if first_pass:
                        MSET(fa[:, 0:2], 0.0, [fa_r])
                    else:
                        CP(fa[:, 0:2], carry_f[:, :, j], ["carry_f"], [fa_r])
                    ACT(fa[:, 2 + cs:2 + ncp], ps[ba[h]][:, cs:ncp], AF.Copy, [("ps", ba[h])], [fa_r])
                    if first_pass:
                        TS(fa[:, 128:130], fa[:, 128:130], flag[:, 0:1], None, ALU.mult, None, [fa_r, "flag"], [fa_r])
                    CP(carry_f[:, :, j], fa[:, ncp:ncp + 2], [fa_r], ["carry_f"])
                    w0 = colv[:, CV_FFW + j:CV_FFW + j + 1]
                    w1 = colv[:, CV_FFW + NJ + j:CV_FFW + NJ + j + 1]
                    w2 = colv[:, CV_FFW + 2 * NJ + j:CV_FFW + 2 * NJ + j + 1]
                    bb_ = colv[:, CV_FFB + j:CV_FFB + j + 1]
                    ACT(fc[:, cs:ncp], fa[:, cs:ncp], AF.Identity, [fa_r, "colv"], [fc_r], scale=w0, bias=bb_)
                    STT(fc[:, cs:ncp], fa[:, 1 + cs:1 + ncp], w1, fc[:, cs:ncp], ALU.mult, ALU.add, [fa_r, fc_r, "colv"], [fc_r])
                    STT(fc[:, cs:ncp], fa[:, 2 + cs:2 + ncp], w2, fc[:, cs:ncp], ALU.mult, ALU.add, [fa_r, fc_r, "colv"], [fc_r])
                    if has_s:
                        fs, fs_r = T()
                        fsv = fs[:, 0:96].rearrange("p (q t) -> p q t", t=6)
                        CP(fsv[:, :, 0:2], hffn[:, j, :, :], ["hffn", PH], [fs_r])
                        ACT(fsv[:, :, 2:6], ps[ba[h]][:, ncp:ncp + 64].rearrange("p (q t) -> p q t", t=4), AF.Copy,
                            [("ps", ba[h])], [fs_r])
                        CP(faSn[:, j, :].rearrange("p (q t) -> p q t", t=2), fsv[:, :, 4:6], [fs_r, PH], ["faSn"])
                        fcs = fc[:, ncp:ncp + 64].rearrange("p (q t) -> p q t", t=4)
                        ACT(fcs, fsv[:, :, 0:4], AF.Identity, [fs_r, "colv"], [fc_r], scale=w0, bias=bb_)
                        STT(fcs, fsv[:, :, 1:5], w1, fcs, ALU.mult, ALU.add, [fs_r, fc_r, "colv"], [fc_r])
                        STT(fcs, fsv[:, :, 2:6], w2, fcs, ALU.mult, ALU.add, [fs_r, fc_r, "colv"], [fc_r])
                    ACT(fc[:, cs:ncols], fc[:, cs:ncols], AF.Gelu_apprx_tanh, [fc_r], [fc_r])
                    TT(yT[:, j, cs:ncols], ps[bg[h]][:, cs:ncols], fc[:, cs:ncols], ALU.mult, [("ps", bg[h]), fc_r, PH], [("yT", j)])

            if stop == "S6" and pi == len(passes) - 1:
                break
            barrier()
            DMA("sp", gfin_b, bc_dram(gfin_h, D), [PH], ["gfin"], "gfin")
            y_reads = [("yT", j) for j in range(NJ)] + [PH]
            for i in range(8):
                banks = [LB(), LB()]
                for (r0, K) in ((0, 16), (16, 16), (32, 11)):
                    lin_item(w_down, r0, i * 256, 256, K, lambda k: yT[:, k, cs:ncols], y_reads, ncols, cs=cs,
                             banks=banks, first=(r0 == 0), last=(r0 == 32))
                for h in range(2):
                    add_to_X(banks[h], 2 * i + h)

            osl = [(s, rows, c0) for (s, rows, c0) in slots if not (first_pass and s == 0)]
            small_i[0] = 0
            ss, ss_r = SM(4)
            ms, ms_r = SM(4)
            rstd, rstd_r = SM(4)
            MSET(ss, 0.0, [ss_r])
            MSET(ms, 1.0, [ms_r])
            for (s, rows, c0) in osl:
                ACT(xs[:rows], X[:rows, s, :], AF.Square, [("X", s), ss_r], ["xs", ss_r], accum_out=ss[:rows, s:s + 1])
            TS(ms, ss, 1.0 / D, EPS, ALU.mult, ALU.add, [ss_r, ms_r], [ms_r])
            ACT(ms, ms, AF.Sqrt, [ms_r], [ms_r])
            RCP(rstd, ms, [ms_r], [rstd_r])
            for (s, rows, c0) in osl:
                STT(X[:rows, s, :], X[:rows, s, :], rstd[:rows, s:s + 1], gfin_b[:rows], ALU.mult, ALU.mult,
                    [("X", s), rstd_r, "gfin"], [("X", s)])
                if rows == 128:
                    t = ptiles[s]
                    DMA("sp", yp[(t - 1) * 128:t * 128, :], X[:, s, :], [("X", s)], [("X", s)], f"X{s}")
                else:
                    DMA("sp", ys, X[0:64, s, :], [("X", s)], [("X", s)], f"X{s}")
                if f"X{s}" not in out_keys:
                    out_keys.append(f"X{s}")
            barrier()
            if has_s:
                for j in range(NJ):
                    if j % 4 == 0:
                        b = AB()
                    TR(ps[b][0:32, (j % 4) * 128:(j % 4 + 1) * 128], faSn[:, j, :], idf[:], ["faSn", "idf", PH], [("ps", b)])
                    if j % 4 == 3 or j == NJ - 1:
                        n = j % 4 + 1
                        j0 = j - (j % 4)
                        CP(stg[0:32, j0 * 128:(j0 + n) * 128],
                           ps[b][0:32, 0:n * 128], [("ps", b), PH], ["stg_nfs"], eng="act" if (j // 4) % 2 else "dve")
                DMA("sp", nfs_o, stg[0:32, 0:DFF], ["stg_nfs", PH], ["onfs"], "onfs")
                out_keys.append("onfs")
            if has_s:
                barrier()

        for half in range(2 if "end" not in SKIP else 0):
            b = AB()
            for c4 in range(4):
                TR(ps[b][0:30, c4 * 128:(c4 + 1) * 128], carry_g[:, half * 4 + c4, :], idf[:], ["carry_g", "idf"], [("ps", b)])
            CP(stg[0:30, 5504 + half * 512:5504 + (half + 1) * 512], ps[b][0:30, :], [("ps", b), PH], ["stg_ncp"])
        if "end" not in SKIP:
            DMA("sp", ncp_o, stg[0:30, 5504:6528], ["stg_ncp", PH], ["oncp"], "oncp")
        out_keys.append("oncp")
        b = AB()
        if "end" not in SKIP:
          TR(ps[b][0:86, 0:128], carry_f[:].rearrange("p t j -> p (t j)"), idf[:], ["carry_f", "idf"], [("ps", b)])
        if "end" not in SKIP:
            CP(stg[0:86, 6528:6656], ps[b][0:86, 0:128], [("ps", b), PH], ["stg_nfp"])
            DMA("sp", nfp_o.rearrange("t (j p) -> (t j) p", p=128), stg[0:86, 6528:6656], ["stg_nfp", PH], ["onfp"], "onfp")
        out_keys.append("onfp")
        out_res = ["omk", "omv", "oncp", "onfp", "WsT", "WsTs", "colv", "lnvg", "lnvb", "gfin", "bsb", "flag"] + [("X", s) for s in range(4)]
        if any(hs for _, hs in passes):
            out_res += ["ovs", "oncs0", "oncs1", "onfs"]
        if debug:
            out_res += ["dbg1", "dbg2", "dbg3"]
        P.op("sp", None, out_res, ())
        P.finalize()
        P.emit()
    return nc


def _prep_inputs(inp):
    f = lambda a: np.ascontiguousarray(np.asarray(a, dtype=np.float32))
    xpr = f(inp["x_prompt"])
    xsa = f(inp["x_sample"])
    colsrc = np.zeros((CV_ROWS, 128), np.float32)
    colsrc[CV_GMIX:CV_GMIX + 16] = f(inp["g_mix"])[0].reshape(16, 128)
    colsrc[CV_GFFN:CV_GFFN + 16] = f(inp["g_ffn"])[0].reshape(16, 128)
    colsrc[CV_GMEM:CV_GMEM + 16] = f(inp["g_mem"])[0].reshape(16, 128)
    colsrc[CV_LNBG:CV_LNBG + 8] = f(inp["ln_b_g"])[0].reshape(8, 128)
    colsrc[CV_LNBB:CV_LNBB + 8] = f(inp["ln_b_b"])[0].reshape(8, 128)
    colsrc[CV_CONVB:CV_CONVB + 8] = f(inp["conv_b"])[0].reshape(8, 128)
    colsrc[CV_CONVW:CV_CONVW + 248] = f(inp["conv_w"])[0].reshape(31 * 8, 128)
    colsrc[CV_FFW:CV_FFW + 129] = f(inp["ffn_conv_w"])[0].reshape(3 * NJ, 128)
    colsrc[CV_FFB:CV_FFB + NJ] = f(inp["ffn_conv_b"])[0].reshape(NJ, 128)
    shared = dict(
        colsrc=colsrc,
        lnvg=f(inp["ln_v_g"]).reshape(1, DA), lnvb=f(inp["ln_v_b"]).reshape(1, DA),
        gfin=f(inp["g_final"]).reshape(1, D), bs=f(inp["b_s"]).reshape(1, 1024),
        ws=f(inp["w_s"])[0],
        idb=np.eye(128).astype(ml_dtypes.bfloat16), idf=np.eye(128, dtype=np.float32),
        tril=np.tril(np.ones((128, 128), np.float32)),
        w_in=f(inp["w_in"])[0], w_pa=f(inp["w_pa"])[0], w_pb=f(inp["w_pb"])[0], w_pc=f(inp["w_pc"])[0],
        w_o=f(inp["w_o"])[0], w_k=f(inp["w_k"])[0], w_v=f(inp["w_v"])[0], w_up=f(inp["w_up"])[0],
        w_down=f(inp["w_down"])[0],
    )
    ck = f(inp["cache_mem_k"])[0].reshape(128, 256, 1024)
    cv = f(inp["cache_mem_v"])[0].reshape(128, 256, 1024)
    sconv = f(inp["state_conv"])[0]
    sffn = f(inp["state_ffn_conv"])[0]
    mem = f(inp["mem_prompt"])
    in_maps = []
    for c in range(NCORES):
        b, half = c // 2, c % 2
        xp = np.zeros((9 * 128, D), np.float32)
        xp[128:] = xpr[b, half * 1024:(half + 1) * 1024]
        if half:
            xp[:128] = xpr[b, 896:1024]
        m = dict(shared)
        m.update(
            xp=xp, xsm=np.ascontiguousarray(xsa[c * 16:(c + 1) * 16].reshape(64, D)), mem=np.ascontiguousarray(mem[b]),
            ck=np.ascontiguousarray(ck[c * 16:(c + 1) * 16]), cvv=np.ascontiguousarray(cv[c * 16:(c + 1) * 16]),
            sconv=np.ascontiguousarray(sconv[c * 16:(c + 1) * 16]),
            sffn=np.ascontiguousarray(sffn[c * 16:(c + 1) * 16].reshape(32, DFF)),
            flag=np.full((128, 1), float(half), np.float32),
        )
        in_maps.append(m)
    return in_maps


_NC_CACHE = {}


def kernel(**inputs):
    in_maps = _prep_inputs(inputs)
    if "nc" not in _NC_CACHE:
        _NC_CACHE["nc"] = build_nc()
    nc = _NC_CACHE["nc"]
    res = run_bass_kernel_spmd(nc, in_maps, core_ids=list(range(NCORES)))
    R = res.results
    y_prompt = np.zeros((4, 2048, D), np.float32)
    y_sample = np.zeros((128, 4, D), np.float32)
    mk = np.zeros((1, 4, 256, 4, 256), np.float32)
    mv = np.zeros((1, 4, 256, 4, 256), np.float32)
    cvp = np.zeros((1, 4, 30, 1024), np.float32)
    ffp = np.zeros((1, 4, 2, DFF), np.float32)
    cvs = np.zeros((1, 128, 30, 1024), np.float32)
    ffs = np.zeros((1, 128, 2, DFF), np.float32)
    vs = np.zeros((1, 128, 4, DA), np.float32)
    for c in range(NCORES):
        b, half = c // 2, c % 2
        r = R[c]
        y_prompt[b, half * 1024:(half + 1) * 1024] = r["yp"]
        y_sample[c * 16:(c + 1) * 16] = r["ys"].reshape(16, 4, D)
        cvs[0, c * 16:(c + 1) * 16] = r["ncs"]
        ffs[0, c * 16:(c + 1) * 16] = r["nfs"].reshape(16, 2, DFF)
        vs[0, c * 16:(c + 1) * 16] = r["vs"].reshape(16, 4, DA)
        if half == 0:
            mk[0, b] = r["mk"].reshape(256, 4, 256)
            mv[0, b] = r["mv"].reshape(256, 4, 256)
        else:
            cvp[0, b] = r["ncp"]
            ffp[0, b] = r["nfp"]
    return (y_prompt, y_sample, mk, mv, cvp, ffp, cvs, ffs, vs)
```

```python
import numpy as np
import ml_dtypes
from contextlib import ExitStack
import concourse.bass as bass
import concourse.mybir as mybir
from concourse.bass_utils import run_bass_kernel_spmd

F32 = mybir.dt.float32
BF16 = mybir.dt.bfloat16
F32R = mybir.dt.float32r
AF = mybir.ActivationFunctionType
ALU = mybir.AluOpType
AX = mybir.AxisListType

ENGS = ("pe", "act", "dve", "pool", "sp")
D = 2048
DA = 1024
DFF = 5504
NJ = 43
EPS = 1e-6
NCORES = 8


class Op:
    __slots__ = ("eng", "fn", "reads", "writes", "dma", "waits", "sig", "signal", "idx")

    def __init__(self, eng, fn, reads, writes, dma):
        self.eng = eng
        self.fn = fn
        self.reads = reads
        self.writes = writes
        self.dma = dma
        self.waits = []
        self.sig = None
        self.signal = False


class Prog:
    def __init__(self, nc, stack):
        self.nc = nc
        self.stack = stack
        self.ops = []
        self.sems = {}
        self.witems = []
        self.lookahead = 3

    def sem(self, key):
        if key not in self.sems:
            self.sems[key] = self.stack.enter_context(self.nc.semaphore("s_" + key))
        return self.sems[key]

    def op(self, eng, fn, reads=(), writes=(), dma=None):
        o = Op(eng, fn, tuple(reads), tuple(writes), dma)
        self.ops.append(o)
        return o

    def wdma(self, eng, fn, writes, dma):
        o = Op(eng, fn, (), tuple(writes), dma)
        self.witems.append((o, len(self.ops)))

    def finalize(self):
        ins_at = {}
        for j, (o, pos) in enumerate(self.witems):
            p = self.witems[max(0, j - self.lookahead)][1]
            ins_at.setdefault(p, []).append(o)
        merged = []
        for i, o in enumerate(self.ops):
            merged.extend(ins_at.get(i, ()))
            merged.append(o)
        merged.extend(ins_at.get(len(self.ops), ()))
        self.ops = merged
        last_w = {}
        readers = {}
        deps_all = []
        for i, o in enumerate(self.ops):
            deps = {}
            for r in o.reads:
                w = last_w.get(r)
                if w is not None:
                    deps[w] = "RAW"
            for r in o.writes:
                w = last_w.get(r)
                if w is not None and w not in deps:
                    deps[w] = "WAW"
                for rd in readers.get(r, ()):
                    if rd not in deps:
                        deps[rd] = "WAR"
            deps.pop(i, None)
            keep = {}
            for d, kind in deps.items():
                p = self.ops[d]
                if p.dma is None and o.dma is None and p.eng == o.eng:
                    if o.eng == "pe":
                        continue
                if p.dma is not None:
                    keep[("dma", p.dma, d)] = d
                else:
                    k = ("eng", p.eng)
                    if k not in keep or keep[k] < d:
                        keep[k] = d
            dl = sorted(set(keep.values()))
            deps_all.append(dl)
            for d in dl:
                self.ops[d].signal = True
            for r in o.reads:
                readers.setdefault(r, []).append(i)
            for r in o.writes:
                last_w[r] = i
                readers[r] = []
        cnt = {}
        for o in self.ops:
            if o.dma is not None:
                k = "d_" + o.dma
                cnt[k] = cnt.get(k, 0) + 16
                o.sig = (k, cnt[k])
            elif o.signal:
                k = "e_" + o.eng
                cnt[k] = cnt.get(k, 0) + 1
                o.sig = (k, cnt[k])
        self.final_cnt = cnt
        seen = {e: {} for e in ENGS}
        for o, dl in zip(self.ops, deps_all):
            need = {}
            for d in dl:
                k, v = self.ops[d].sig
                if k.startswith("d_const"):
                    v = cnt[k]
                if v > need.get(k, 0):
                    need[k] = v
            s = seen[o.eng]
            for k, v in need.items():
                if s.get(k, 0) >= v:
                    continue
                s[k] = v
                o.waits.append((k, v))

    def emit(self):
        nc = self.nc
        for k in self.final_cnt:
            self.sem(k)
        per = {e: [o for o in self.ops if o.eng == e] for e in ENGS}
        names = {"pe": "tensor", "act": "scalar", "dve": "vector", "pool": "gpsimd", "sp": "sync"}
        with nc.Block() as block:
            for e in ENGS:
                if not per[e]:
                    continue

                def body(eng, lst=per[e]):
                    for o in lst:
                        for k, v in o.waits:
                            eng.wait_ge(self.sems[k], v)
                        if o.fn is None:
                            continue
                        ins = o.fn(eng)
                        if o.sig is not None:
                            ins.then_inc(self.sems[o.sig[0]], 16 if o.dma is not None else 1)

                getattr(block, names[e])(body)


CV_GMIX, CV_GFFN, CV_GMEM = 0, 16, 32
CV_LNBG, CV_LNBB, CV_CONVB = 48, 56, 64
CV_CONVW = 72
CV_FFW = 320
CV_FFB = 449
CV_ROWS = 512

SKIP = set()
PASSES = [
    ([0, 1, 2], True),
    ([3, 4, 5], False),
    ([6, 7, 8], False),
]


def build_nc(passes=PASSES, debug=False, stop=None):
    nc = bass.Bass("TRN2", target_bir_lowering=False)

    def din(n, s, d=F32):
        return nc.dram_tensor(n, list(s), d, kind="ExternalInput")

    def dout(n, s, d=F32):
        return nc.dram_tensor(n, list(s), d, kind="ExternalOutput").ap()

    xp = din("xp", [9 * 128, D]).ap()
    xsm = din("xsm", [64, D]).ap()
    mem = din("mem", [256, D]).ap()
    ck = din("ck", [16, 256, 1024]).ap()
    cvv = din("cvv", [16, 256, 1024]).ap()
    sconv = din("sconv", [16, 30, 1024]).ap()
    sffn = din("sffn", [32, DFF]).ap()
    flag_d = din("flag", [128, 1]).ap()
    colsrc = din("colsrc", [CV_ROWS, 128]).ap()
    lnvg_h = din("lnvg", [1, DA])
    lnvb_h = din("lnvb", [1, DA])
    gfin_h = din("gfin", [1, D])
    bs_h = din("bs", [1, 8 * 128])
    ws_d = din("ws", [8, 128, 128]).ap()
    idb_d = din("idb", [128, 128], BF16).ap()
    idf_d = din("idf", [128, 128]).ap()
    tril_d = din("tril", [128, 128]).ap()
    w_in = din("w_in", [D, 11264]).ap()
    w_pa = din("w_pa", [DA, D]).ap()
    w_pb = din("w_pb", [DA, D]).ap()
    w_pc = din("w_pc", [DA, D]).ap()
    w_o = din("w_o", [D, D]).ap()
    w_k = din("w_k", [D, 1024]).ap()
    w_v = din("w_v", [D, 1024]).ap()
    w_up = din("w_up", [D, 2 * DFF]).ap()
    w_down = din("w_down", [DFF, D]).ap()

    yp = dout("yp", [1024, D])
    ys = dout("ys", [64, D])
    mk_o = dout("mk", [256, 1024])
    mv_o = dout("mv", [256, 1024])
    ncp_o = dout("ncp", [30, 1024])
    nfp_o = dout("nfp", [2, DFF])
    ncs_o = dout("ncs", [16, 30, 1024])
    nfs_o = dout("nfs", [32, DFF])
    vs_o = dout("vs", [64, DA])
    out_keys = []

    TMAX = 448
    with ExitStack() as st:
        P = Prog(nc, st)

        def sb(n, s, d=F32):
            return st.enter_context(nc.sbuf_tensor(n, list(s), d))

        X = sb("X", [128, 4, D])
        hT = sb("hT", [128, 16, TMAX], BF16)
        AR_KB = 68
        arena = sb("arena", [128, AR_KB * 512], BF16)
        NW = 7
        wslots = [sb(f"wsl{i}", [128, 16, 256], BF16) for i in range(NW)]
        P.lookahead = NW - 1
        bsb = sb("bsb", [128, 8, 128])
        WsT = sb("WsT", [128, 8, 128], BF16)
        WsTs = sb("WsTs", [64, 8, 64], BF16)
        colv = sb("colv", [128, CV_ROWS])
        idb = sb("idb_s", [128, 128], BF16)
        idf = sb("idf_s", [128, 128])
        tril = sb("tril_s", [128, 128])
        ones_f = sb("ones_f", [128, 128])
        ones_b = sb("ones_b", [128, 128], BF16)
        flag = sb("flag_s", [128, 1])
        KT = sb("KT", [128, 8, 256], BF16)
        Vb = sb("Vb", [128, 2, 1024], BF16)
        xs = sb("xs", [128, D], BF16)
        NTMP = 6
        tmps = [sb(f"tmp{i}", [128, 452]) for i in range(NTMP)]
        carry_g = sb("carry_g", [128, 8, 30])
        carry_f = sb("carry_f", [128, 2, NJ])
        small = sb("small", [128, 128])
        macc = [xs[:, i * 896:(i + 1) * 896].bitcast(F32) for i in range(2)]
        ptmp = []
        NDG = 4
        dgs = [sb(f"dg{i}", [128, 128]) for i in range(NDG)]
        ps = [st.enter_context(nc.psum_tensor(f"ps{i}", [128, 512], F32)) for i in range(8)]
        psb = [p[:].bitcast(BF16) for p in ps]

        def ar(kb0, nbytes, dt, pattern=None, **kw):
            e0 = int(kb0 * 512)
            v = arena[:, e0:e0 + nbytes // 2]
            if dt in (F32, F32R):
                v = v.bitcast(dt)
            if pattern:
                v = v.rearrange(pattern, **kw)
            return v

        uT = ar(0, 8 * TMAX * 2, BF16, "p (k t) -> p k t", k=8)
        qT = ar(7, 8 * TMAX * 2, BF16, "p (k t) -> p k t", k=8)
        bT = ar(14, 8 * TMAX * 2, BF16, "p (k t) -> p k t", k=8)
        R0 = 21
        vtok = ar(R0, 4 * DA * 4, F32, "p (s d) -> p s d", s=4)
        vn = ar(R0 + 16, 4 * DA * 2, BF16, "p (s d) -> p s d", s=4)
        vsf = ar(R0 + 24, DA * 4, F32)
        GW = 30 + 384
        gT = ar(R0, 8 * GW * 4, F32, "p (c t) -> p c t", c=8)
        gTs = ar(R0 + 13, 8 * 16 * 34 * 4, F32, "p (c q t) -> p c q t", c=8, q=16)
        gTr = ar(R0, 8 * GW * 4, F32R, "p (c t) -> p c t", c=8)
        gTsr = ar(R0 + 13, 8 * 16 * 34 * 4, F32R, "p (c q t) -> p c q t", c=8, q=16)
        cv = ar(R0 + 30, 8 * TMAX * 4, F32, "p (c t) -> p c t", c=8)
        sc = ar(R0 + 30, 4 * 1024 * 4, F32, "p (j c) -> p j c", j=4)
        Pm = [ar(R0 + 2 * i, 4 * 256 * 2, BF16, "p (h m) -> p h m", h=4) for i in range(2)]
        PT = [ar(R0 + 4 + 2 * i, 8 * 128 * 2, BF16, "p (k t) -> p k t", k=8) for i in range(2)]
        Kseq = [ar(R0 + 8 + 4 * i, 2 * 1024 * 2, BF16, "p (m e) -> p m e", m=2) for i in range(2)]
        Vseq = [ar(R0 + 16 + 4 * i, 2 * 1024 * 2, BF16, "p (m e) -> p m e", m=2) for i in range(2)]
        KTs = [ar(R0 + 24 + 4 * i, 8 * 256 * 2, BF16, "p (k m) -> p k m", k=8) for i in range(2)]
        expT = ar(R0 + 32, 2 * 256 * 2, BF16, "p (m c) -> p m c", m=2)
        cT = ar(61, 8 * TMAX * 2, BF16, "p (k t) -> p k t", k=8)
        mixT = ar(R0, 16 * TMAX * 2, BF16, "p (k t) -> p k t", k=16)
        yT = ar(0, NJ * TMAX * 2, BF16, "p (k t) -> p k t", k=NJ)
        hffn = ar(44, NJ * 32 * 4, F32, "p (j q t) -> p j q t", j=NJ, q=16)
        faSn = ar(50, NJ * 32 * 4, F32, "p (j r) -> p j r", j=NJ)
        stg = ar(0, 44 * 1024, F32)
        sst = ar(56, 8 * 1024, F32)
        gst = ar(R0 + 30, 4 * 1024, F32)
        wsn = ar(21, 8 * 128 * 4, F32, "p (g s) -> p g s", g=8)
        wsns = ar(25, 8 * 64 * 4, F32, "p (g s) -> p g s", g=8)
        cst = ar(27, 4 * 128 * 4, F32, "p (j c) -> p j c", j=4)
        memx = ar(29, 2 * D * 4, F32, "p (m d) -> p m d", m=2)
        kst = ar(45, 2 * 1024 * 4, F32, "p (m e) -> p m e", m=2)
        mT = ar(53, 16 * 256 * 2, BF16, "p (k m) -> p k m", k=16)

        lnvg_b = ar(49, DA * 4, F32)
        lnvb_b = ar(53, DA * 4, F32)
        gfin_b = ar(56, D * 4, F32)
        PH = ("ph",)

        def RD(reads, *aps):
            r = list(reads)
            if PH not in r and any(getattr(a, "name", None) == "arena" for a in aps):
                r.append(PH)
            return r

        def ACT(out, in_, func, reads, writes, **kw):
            P.op("act", lambda e: e.activation(out=out, in_=in_, func=func, **kw), RD(reads, out, in_), writes)

        def MM(out, lhsT, rhs, start, stop, reads, writes):
            P.op("pe", lambda e: e.matmul(out, lhsT=lhsT, rhs=rhs, start=start, stop=stop), RD(reads, lhsT, rhs), writes)

        def TR(out, in_, ident, reads, writes):
            P.op("pe", lambda e: e.transpose(out=out, in_=in_, identity=ident), RD(reads, in_), writes)

        def TT(out, in0, in1, op, reads, writes, eng="dve"):
            P.op(eng, lambda e: e.tensor_tensor(out=out, in0=in0, in1=in1, op=op), RD(reads, out, in0, in1), writes)

        def TS(out, in0, s1, s2, op0, op1, reads, writes, eng="dve"):
            if op1 is None:
                P.op(eng, lambda e: e.tensor_scalar(out=out, in0=in0, scalar1=s1, scalar2=None, op0=op0),
                     RD(reads, out, in0), writes)
            else:
                P.op(eng, lambda e: e.tensor_scalar(out=out, in0=in0, scalar1=s1, scalar2=s2, op0=op0, op1=op1),
                     RD(reads, out, in0), writes)

        def STT(out, in0, scalar, in1, op0, op1, reads, writes, eng="dve"):
            P.op(eng, lambda e: e.scalar_tensor_tensor(out=out, in0=in0, scalar=scalar, in1=in1, op0=op0, op1=op1),
                 RD(reads, out, in0, in1), writes)

        def CP(out, in_, reads, writes, eng="dve"):
            if eng == "act":
                P.op("act", lambda e: e.activation(out=out, in_=in_, func=AF.Copy), RD(reads, out, in_), writes)
            else:
                P.op(eng, lambda e: e.tensor_copy(out=out, in_=in_), RD(reads, out, in_), writes)

        def MSET(ap, val, writes, eng="dve"):
            P.op(eng, lambda e: e.memset(ap, val), RD((), ap), writes)

        def RCP(out, in_, reads, writes):
            P.op("dve", lambda e: e.reciprocal(out=out, in_=in_), RD(reads, out, in_), writes)

        def DMA(eng, out, in_, reads, writes, key):
            P.op(eng, lambda e: e.dma_start(out=out, in_=in_), RD(reads, out, in_), writes, dma=key)

        def barrier():
            P.op("sp", lambda e: e.nop(), (), [PH])

        tmp_i = [0]

        def T():
            i = tmp_i[0] % NTMP
            tmp_i[0] += 1
            return tmps[i], ("tmp", i)

        lin_i = [0]

        def LB():
            b = lin_i[0] % 4
            lin_i[0] += 1
            return b

        aux_i = [0]

        def AB():
            b = 4 + aux_i[0] % 4
            aux_i[0] += 1
            return b

        small_i = [0]

        def SM(n):
            a = small_i[0]
            small_i[0] += n
            assert small_i[0] <= 128
            return small[:, a:a + n], ("small", a)

        wi = [0]

        def witem(W, r0, nk, c0, ncol):
            s = wi[0] % NW
            wi[0] += 1
            src = W[r0 * 128:(r0 + nk) * 128, c0:c0 + ncol].rearrange("(k p) c -> p k c", p=128)
            dst = wslots[s][:, 0:nk, 0:ncol]
            P.wdma("pool", lambda e: e.dma_start(out=dst, in_=src), [("w", s)], f"w{s}")
            return s

        def bc_dram(h, n):
            return bass.AP(h, 0, [[0, 128], [1, n]])

        for s_, t_ in enumerate(passes[0][0]):
            DMA("sp", X[:, s_, :], xp[t_ * 128:(t_ + 1) * 128, :], (), [("X", s_)], f"X{s_}")
        if passes[0][1]:
            DMA("sp", X[0:64, len(passes[0][0]), :], xsm, (), [("X", len(passes[0][0]))], f"X{len(passes[0][0])}")
        if any(hs for _, hs in passes):
            DMA("sp", ncs_o[:, 0:26, :], sconv[:, 4:30, :], (), ["oncs0"], "oncs0")
        DMA("sp", idb[:], idb_d, (), ["idb"], "const")
        DMA("sp", idf[:], idf_d, (), ["idf"], "const")
        DMA("sp", tril[:], tril_d, (), ["tril"], "const")
        DMA("sp", flag[:], flag_d, (), ["flag"], "const")
        if "bc" not in SKIP:
            DMA("sp", bsb[:].rearrange("p g t -> p (g t)"), bc_dram(bs_h, 1024), (), ["bsb"], "const")
        for j in range(4):
            DMA("sp", cst[:, j, :], colsrc[j * 128:(j + 1) * 128, :], [PH], [("cst", j)], "const")
        DMA("sp", wsn, ws_d.rearrange("g t s -> t g s"), [PH], ["wsn"], "const")
        for i_ in range(NTMP):
            MSET(tmps[i_][:], 0.0, [("tmp", i_)])
        MSET(ones_f[:], 1.0, ["ones_f"])
        MSET(ones_b[:], 1.0, ["ones_b"])
        MSET(carry_g[:], 0.0, ["carry_g"])
        MSET(carry_f[:], 0.0, ["carry_f"])
        wsq = [("wsns", q) for q in range(16)]
        MSET(wsns[0:64], 0.0, wsq)
        for q in range(16 if "wsnsdma" not in SKIP else 0):
            DMA("sp", wsns[4 * q:4 * q + 4, :, 4 * q:4 * q + 4], ws_d[:, 0:4, 0:4].rearrange("g t s -> t g s"),
                [PH], [("wsns", q)], "const2")
        b = AB()
        for j in range(4):
            TR(ps[b][:, j * 128:(j + 1) * 128], cst[:, j, :], idf[:], [("cst", j), "idf", PH], [("ps", b)])
        CP(colv[:], ps[b][:], [("ps", b)], ["colv"])
        TT(wsn, wsn, tril[:].unsqueeze(1).to_broadcast([128, 8, 128]), ALU.mult, ["wsn", "tril", PH], ["wsn"])
        for hh in range(2):
            b = AB()
            for g in range(4):
                TR(ps[b][:, g * 128:(g + 1) * 128], wsn[:, hh * 4 + g, :], idf[:], ["wsn", "idf", PH], [("ps", b)])
            CP(WsT[:, hh * 4:hh * 4 + 4, :], ps[b][:].rearrange("p (g t) -> p g t", g=4), [("ps", b)], ["WsT"])
        TT(wsns[0:64], wsns[0:64], tril[0:64, 0:64].unsqueeze(1).to_broadcast([64, 8, 64]), ALU.mult, wsq + ["tril", PH], ["wsns"])
        b = AB()
        for g in range(8 if "wsnstr" not in SKIP else 0):
            TR(ps[b][0:64, g * 64:(g + 1) * 64], wsns[0:64, g, :], idf[0:64, 0:64], ["wsns", "idf", PH], [("ps", b)])
        CP(WsTs[:], ps[b][0:64, :].rearrange("p (g t) -> p g t", g=8), [("ps", b)], ["WsTs"])

        def norm_T(srcs, gidx, dst, dst_res, tag):
            small_i[0] = 0
            ss, ss_r = SM(4)
            ms, ms_r = SM(4)
            rstd, rstd_r = SM(4)
            MSET(ss, 0.0, [ss_r])
            for i, (src, rows, c0, res) in enumerate(srcs):
                ACT(xs[:rows], src, AF.Square, [res, ss_r], ["xs", ss_r], accum_out=ss[:rows, i:i + 1])
            TS(ms, ss, 1.0 / D, EPS, ALU.mult, ALU.add, [ss_r], [ms_r])
            ACT(ms, ms, AF.Sqrt, [ms_r], [ms_r])
            RCP(rstd, ms, [ms_r], [rstd_r])
            for i, (src, rows, c0, res) in enumerate(srcs):
                ACT(xs[:rows], src, AF.Copy, [res, rstd_r], ["xs"], scale=rstd[:rows, i:i + 1])
                for hh in range(2):
                    b = AB()
                    pv = psb[b].rearrange("p (k t) -> p k t", k=8)
                    for k in range(8):
                        kk = hh * 8 + k
                        TR(pv[:, k, 0:rows], xs[:rows, kk * 128:(kk + 1) * 128], idb[:rows, :rows],
                           ["xs", "idb"], [("ps", b)])
                    TT(dst[:, hh * 8:hh * 8 + 8, c0:c0 + rows], pv[:, :, 0:rows],
                       colv[:, gidx + hh * 8:gidx + hh * 8 + 8].unsqueeze(2).to_broadcast([128, 8, rows]),
                       ALU.mult, [("ps", b), "colv"], [dst_res(i)])

        def lin_item(W, r0, c0, ncol, K, act_fn, act_reads, ncols, banks=None, first=True, last=True, cs=0):
            s = witem(W, r0, K, c0, ncol)
            nh = ncol // 128
            if banks is None:
                banks = [LB() for _ in range(nh)]
            for h in range(nh):
                for k in range(K):
                    MM(ps[banks[h]][:, cs:ncols], wslots[s][:, k, h * 128:(h + 1) * 128], act_fn(r0 + k),
                       first and k == 0, last and k == K - 1, [("w", s)] + act_reads, [("ps", banks[h])])
            return banks

        for m in range(2 if "kv" not in SKIP else 0):
            DMA("sp", memx[:, m, :], mem[m * 128:(m + 1) * 128, :], [PH], [("memx", m)], f"memx{m}")
        if "kv" not in SKIP:
            norm_T([(memx[:, m, :], 128, m * 128, ("memx", m)) for m in range(2)], CV_GMEM, mT, lambda i: "mT", "m")
        for which, W, okey, o_ap in ((("k", w_k, "omk", mk_o), ("v", w_v, "omv", mv_o)) if ("kv" not in SKIP and "kvlin" not in SKIP) else ()):
            for i in range(4):
                banks = lin_item(W, 0, i * 256, 256, 16, lambda k: mT[:, k, :], ["mT", PH], 256)
                for h in range(2):
                    ec = 2 * i + h
                    t, tr = T()
                    CP(t[:, 0:256], ps[banks[h]][:, 0:256], [("ps", banks[h])], [tr], eng="act")
                    if which == "k":
                        CP(KT[:, ec, :], t[:, 0:256], [tr], ["KT"])
                    if "kvtr" in SKIP:
                        continue
                    b = AB()
                    for m in range(2):
                        TR(ps[b][:, m * 128:(m + 1) * 128], t[:, m * 128:(m + 1) * 128], idf[:], [tr, "idf"], [("ps", b)])
                    CP(kst[:, :, ec * 128:(ec + 1) * 128], ps[b][:, 0:256].rearrange("p (m e) -> p m e", m=2),
                       [("ps", b), PH], ["kst"], eng="act")
            if which == "v" and "kvtr" not in SKIP:
                CP(Vb[:].rearrange("p m e -> p (m e)"), kst.rearrange("p m e -> p (m e)"), ["kst", PH], ["Vb"])
            if "kvout" not in SKIP:
                DMA("sp", o_ap.rearrange("(m p) e -> p m e", p=128), kst, ["kst", PH], [okey], okey)
            out_keys.append(okey)
        barrier()

        for pi, (ptiles, has_s) in enumerate(passes):
            ntp = len(ptiles)
            ncp = ntp * 128
            ncols = ncp + (64 if has_s else 0)
            nsl = ntp + (1 if has_s else 0)
            slots = []
            for s in range(ntp):
                slots.append((s, 128, s * 128))
            if has_s:
                slots.append((ntp, 64, ncp))
            hT_reads = [("hT", s) for s in range(nsl)]
            first_pass = ptiles[0] == 0

            if stop == "setup":
                break
            if pi > 0:
                for s, t in enumerate(ptiles):
                    DMA("sp", X[:, s, :], xp[t * 128:(t + 1) * 128, :], (), [("X", s)], f"X{s}")
                if has_s:
                    DMA("sp", X[0:64, ntp, :], xsm, (), [("X", ntp)], f"X{ntp}")
            norm_T([(X[:rows, s, :], rows, c0, ("X", s)) for (s, rows, c0) in slots], CV_GMIX, hT,
                   lambda i: ("hT", i), "n1")
            cs = 96 if first_pass else 0
            hfn = lambda k: hT[:, k, cs:ncols]
            hfn_full = lambda k: hT[:, k, 0:ncols]

            if stop == "S1" and pi == len(passes) - 1:
                break
            DMA("sp", lnvg_b, bc_dram(lnvg_h, DA), [PH], ["lnvg"], "lnvg")
            DMA("sp", lnvb_b, bc_dram(lnvb_h, DA), [PH], ["lnvb"], "lnvb")
            for i in range(4):
                banks = lin_item(w_in, 0, 1024 + i * 256, 256, 16, hfn_full, hT_reads, ncols)
                for h in range(2):
                    d = 2 * i + h
                    t, tr = T()
                    ACT(t[:, 0:ncols], ps[banks[h]][:, 0:ncols], AF.Gelu_apprx_tanh, [("ps", banks[h])], [tr])
                    b = AB()
                    for (s, rows, c0) in slots:
                        TR(ps[b][:rows, s * 128:(s + 1) * 128], t[:, c0:c0 + rows], idf[:], [tr, "idf"], [("ps", b)])
                    CP(vtok[:, 0:ntp, d * 128:(d + 1) * 128], ps[b][:, 0:ncp].rearrange("p (s e) -> p s e", s=ntp),
                       [("ps", b), PH], [("vtok", d)])
                    if has_s:
                        CP(vtok[0:64, ntp, d * 128:(d + 1) * 128], ps[b][0:64, ncp:ncp + 128],
                           [("ps", b), PH], [("vtok", d)])
            for i in range(4):
                banks = lin_item(w_in, 0, i * 256, 256, 16, hfn, hT_reads, ncols, cs=cs)
                for h in range(2):
                    d = 2 * i + h
                    ACT(uT[:, d, cs:ncols], ps[banks[h]][:, cs:ncols], AF.Gelu_apprx_tanh, [("ps", banks[h]), PH], [("uT", d)])
            zb_pre = (lin_item(w_in, 0, 2048, 256, 16, hfn, hT_reads, ncols, cs=cs),
                      lin_item(w_in, 0, 3072, 256, 16, hfn, hT_reads, ncols, cs=cs))
            vt_reads = [("vtok", d) for d in range(8)] + [PH]
            small_i[0] = 16
            st6, st6_r = SM(nsl * 12)
            mvv, mv_r = SM(nsl * 2)
            rs_, rs_r = SM(nsl)
            MSET(mvv, 1.0, [mv_r])
            for (s, rows, c0) in slots:
                for hh in range(2):
                    P.op("dve", lambda e, o_=st6[:rows, s * 12 + hh * 6:s * 12 + hh * 6 + 6],
                         i_=vtok[:rows, s, hh * 512:(hh + 1) * 512]: e.bn_stats(out=o_, in_=i_), vt_reads, [st6_r])
                P.op("dve", lambda e, o_=mvv[:rows, 2 * s:2 * s + 2], i_=st6[:rows, s * 12:s * 12 + 12]: e.bn_aggr(out=o_, in_=i_),
                     [st6_r], [mv_r])
            mvw = mvv.rearrange("p (s two) -> p s two", two=2)
            TS(rs_, mvw[:, :, 1], EPS, None, ALU.add, None, [mv_r], [rs_r])
            ACT(rs_, rs_, AF.Sqrt, [rs_r], [rs_r])
            RCP(rs_, rs_, [rs_r], [rs_r])
            for (s, rows, c0) in slots:
                TS(vtok[:rows, s, :], vtok[:rows, s, :], mvv[:rows, 2 * s:2 * s + 1], rs_[:rows, s:s + 1],
                   ALU.subtract, ALU.mult, vt_reads + [mv_r, rs_r], [("vtk2", s)])
                TT(vtok[:rows, s, :], vtok[:rows, s, :], lnvg_b[:rows], ALU.mult, [("vtk2", s), "lnvg", PH], [("vtk2", s)])
                if rows == 128:
                    TT(vn[:, s, :], vtok[:, s, :], lnvb_b, ALU.add, [("vtk2", s), "lnvb", PH], [("vn", s)])
                else:
                    TT(vsf[:rows], vtok[:rows, s, :], lnvb_b[:rows], ALU.add, [("vtk2", s), "lnvb", PH], ["vsf"])
                    CP(vn[:rows, s, :], vsf[:rows], ["vsf", PH], [("vn", s)], eng="act")
                    DMA("sp", vs_o, vsf[0:64], ["vsf", PH], ["ovs"], "ovs")
                    out_keys.append("ovs")
            for (s, rows, c0) in slots:
                tl = cs if (first_pass and s == 0) else 0
                for gh in range(2):
                    b = AB()
                    pv = ps[b][:].rearrange("p (g t) -> p g t", g=4)
                    for g4 in range(4):
                        g = gh * 4 + g4
                        if rows == 128:
                            MM(pv[:, g4, tl:128], vn[:, s, g * 128:(g + 1) * 128], WsT[:, g, tl:128], True, True,
                               [("vn", s), "WsT", PH], [("ps", b)])
                        else:
                            MM(pv[:, g4, 0:64], vn[0:64, s, g * 128:(g + 1) * 128], WsTs[:, g, :], True, True,
                               [("vn", s), "WsTs", PH], [("ps", b)])
                    ur = [("uT", gh * 4 + g4) for g4 in range(4)]
                    if rows == 128:
                        for g2 in range(2):
                            tw, twr = T()
                            twv = tw[:, 0:256].rearrange("p (g t) -> p g t", g=2)
                            ga = gh * 4 + g2 * 2
                            TT(twv[:, :, tl:128], pv[:, g2 * 2:g2 * 2 + 2, tl:128], bsb[:, ga:ga + 2, tl:128], ALU.add, [("ps", b), "bsb"], [twr])
                            TT(uT[:, ga:ga + 2, c0 + tl:c0 + 128], twv[:, :, tl:128], uT[:, ga:ga + 2, c0 + tl:c0 + 128], ALU.mult,
                               [twr, ("uT", ga), ("uT", ga + 1), PH], [("uT", ga), ("uT", ga + 1)])
                    else:
                        tw, twr = T()
                        twv = tw[:, 0:256].rearrange("p (g q t) -> p g q t", g=4, q=16)
                        ga = gh * 4
                        TT(twv, pv[:, :, 0:64].rearrange("p g (q t) -> p g q t", t=4),
                           bsb[:, ga:ga + 4, 0:4].unsqueeze(2).to_broadcast([128, 4, 16, 4]), ALU.add, [("ps", b), "bsb"], [twr])
                        TT(uT[:, ga:ga + 4, c0:c0 + 64], tw[:, 0:256].rearrange("p (g t) -> p g t", g=4),
                           uT[:, ga:ga + 4, c0:c0 + 64], ALU.mult, [twr, PH] + ur, ur)
            if debug and pi == 0:
                d1 = nc.dram_tensor("dbg_aT", [128, 8, 384], BF16, kind="ExternalOutput").ap()
                d2 = nc.dram_tensor("dbg_vn", [128, 3, DA], BF16, kind="ExternalOutput").ap()
                d3 = nc.dram_tensor("dbg_WsT", [128, 1024], BF16, kind="ExternalOutput").ap()
                DMA("sp", d1, uT[:, :, 0:384], [("uT", k) for k in range(8)] + [PH], ["dbg1"], "dbg1")
                DMA("sp", d2, vn[:, 0:3, :], [("vn", k) for k in range(3)] + [PH], ["dbg2"], "dbg2")
                DMA("sp", d3, WsT[:].rearrange("p g t -> p (g t)"), ["WsT"], ["dbg3"], "dbg3")
            barrier()
            if stop == "A" and pi == len(passes) - 1:
                break

            if first_pass:
                MSET(gT[:, :, 0:30 + cs], 0.0, [("gT", c) for c in range(8)] + [PH])
            else:
                CP(gT[:, :, 0:30], carry_g[:], ["carry_g", PH], [("gT", c) for c in range(8)])
            if has_s:
                for j in range(4):
                    DMA("sp", sc[0:120, j, :], sconv[4 * j:4 * j + 4].rearrange("q r c -> (q r) c"), [PH], [("sc", j)], f"sc{j}")
                for c in range(8):
                    b = AB()
                    for j in range(4):
                        TR(ps[b][:, j * 120:(j + 1) * 120], sc[0:120, j, c * 128:(c + 1) * 128], idf[0:120, 0:120],
                           [("sc", j), "idf", PH], [("ps", b)])
                    CP(gTs[:, c, :, 0:30], ps[b][:, 0:480].rearrange("p (q r) -> p q r", r=30), [("ps", b), PH], [("gTsh", c)],
                       eng="act" if c % 2 else "dve")
                P.op("sp", lambda e: e.nop(), [PH], [("sc", j) for j in range(4)] + [("phcv",)])
            for i in range(4):
                if i == 0:
                    ba, bb = zb_pre
                else:
                    ba = lin_item(w_in, 0, 2048 + i * 256, 256, 16, hfn, hT_reads, ncols, cs=cs)
                    bb = lin_item(w_in, 0, 3072 + i * 256, 256, 16, hfn, hT_reads, ncols, cs=cs)
                for h in range(2):
                    c = 2 * i + h
                    t, tr = T()
                    ACT(t[:, cs:ncols], ps[bb[h]][:, cs:ncols], AF.Sigmoid, [("ps", bb[h])], [tr])
                    TT(gT[:, c, 30 + cs:30 + ncp], ps[ba[h]][:, cs:ncp], t[:, cs:ncp], ALU.mult, [("ps", ba[h]), tr, PH], [("gT", c)])
                    if has_s:
                        TT(gTs[:, c, :, 30:34], ps[ba[h]][:, ncp:ncp + 64].rearrange("p (q t) -> p q t", t=4),
                           t[:, ncp:ncp + 64].rearrange("p (q t) -> p q t", t=4), ALU.mult, [("ps", ba[h]), tr, PH], [("gTs", c)])
                    if first_pass:
                        TS(gT[:, c, 126:158], gT[:, c, 126:158], flag[:, 0:1], None, ALU.mult, None, [("gT", c), "flag", PH], [("gT", c)])
            cvr = [("phcv",), PH]
            NPE = 11
            KD = 31 - NPE
            dgi = 0
            for c in range(8):
                ACT(cv[:, c, cs:ncp], gT[:, c, cs:ncp], AF.Identity, [("gT", c), "colv"] + cvr, [("cv", c)],
                    scale=colv[:, CV_CONVW + c:CV_CONVW + c + 1], bias=colv[:, CV_CONVB + c:CV_CONVB + c + 1])
                if has_s:
                    ACT(cv[:, c, ncp:ncp + 64].rearrange("p (q t) -> p q t", t=4), gTs[:, c, :, 0:4], AF.Identity,
                        [("gTs", c), ("gTsh", c), "colv"] + cvr, [("cvs", c)],
                        scale=colv[:, CV_CONVW + c:CV_CONVW + c + 1], bias=colv[:, CV_CONVB + c:CV_CONVB + c + 1])
            pebanks = []
            GC = 2 if has_s else 4
            for c in range(8):
                gi_ = c % GC
                b = 4 + (2 * gi_ if has_s else gi_)
                b2 = 5 + 2 * gi_ if has_s else None
                pebanks.append((b, b2))
                for k in range(KD, 31):
                    dg = dgs[dgi % NDG]
                    dgr = ("dg", dgi % NDG)
                    dgi += 1
                    wc = colv[:, CV_CONVW + k * 8 + c:CV_CONVW + k * 8 + c + 1]
                    ACT(dg[:], idf[:], AF.Copy, ["idf", "colv"], [dgr], scale=wc)
                    MM(ps[b][:, cs:ncp], dg[:], gT[:, c, cs + k:k + ncp], k == KD, k == 30,
                       [dgr, ("gT", c)] + cvr, [("ps", b)])
                    if has_s:
                        MM(ps[b2][:, 0:64].rearrange("p (q t) -> p q t", t=4), dg[:], gTs[:, c, :, k:k + 4],
                           k == KD, k == 30, [dgr, ("gTs", c), ("gTsh", c)] + cvr, [("ps", b2)])
                if c % GC == GC - 1:
                    for k in range(1, KD):
                        for cc in range(c - GC + 1, c + 1):
                            wc = colv[:, CV_CONVW + k * 8 + cc:CV_CONVW + k * 8 + cc + 1]
                            STT(cv[:, cc, cs:ncp], gT[:, cc, cs + k:k + ncp], wc, cv[:, cc, cs:ncp], ALU.mult, ALU.add,
                                [("gT", cc), ("cv", cc), "colv", PH], [("cv", cc)])
                            if has_s:
                                cvs = cv[:, cc, ncp:ncp + 64].rearrange("p (q t) -> p q t", t=4)
                                STT(cvs, gTs[:, cc, :, k:k + 4], wc, cvs, ALU.mult, ALU.add,
                                    [("gTs", cc), ("gTsh", cc), ("cvs", cc), "colv", PH], [("cvs", cc)])
                    for cc in range(c - GC + 1, c + 1):
                        bb, bb2 = pebanks[cc]
                        TT(cv[:, cc, cs:ncp], cv[:, cc, cs:ncp], ps[bb][:, cs:ncp], ALU.add, [("cv", cc), ("ps", bb), PH], [("cv", cc)])
                        if has_s:
                            TT(cv[:, cc, ncp:ncp + 64], cv[:, cc, ncp:ncp + 64], ps[bb2][:, 0:64], ALU.add,
                               [("cvs", cc), ("ps", bb2), PH], [("cvs", cc)])
            for i in range(4):
                banks = lin_item(w_in, 0, 4096 + i * 256, 256, 16, hfn, hT_reads, ncols, cs=cs)
                for h in range(2):
                    ec = 2 * i + h
                    CP(qT[:, ec, cs:ncols], ps[banks[h]][:, cs:ncols], [("ps", banks[h]), PH], [("qT", ec)], eng="act")
            CP(carry_g[:], gT[:, :, ncp:ncp + 30], [("gT", c) for c in range(8)] + [PH], ["carry_g"])
            b1, b2 = AB(), AB()
            for c in range(8):
                t, tr = T()
                ACT(t[:, cs:ncols], cv[:, c, cs:ncols], AF.Square, [("cv", c), ("cvs", c), PH], [tr])
                MM(ps[b1][:, cs:ncols], ones_f[:], cv[:, c, cs:ncols], c == 0, c == 7, [("cv", c), ("cvs", c), "ones_f", PH], [("ps", b1)])
                MM(ps[b2][:, cs:ncols], ones_f[:], t[:, cs:ncols], c == 0, c == 7, [tr, "ones_f"], [("ps", b2)])
            mean, mean_r = T()
            msq, msq_r = T()
            rsb, rsb_r = T()
            TS(mean[:, cs:ncols], ps[b1][:, cs:ncols], 1.0 / DA, None, ALU.mult, None, [("ps", b1)], [mean_r])
            TT(msq[:, cs:ncols], mean[:, cs:ncols], mean[:, cs:ncols], ALU.mult, [mean_r], [msq_r])
            STT(rsb[:, cs:ncols], ps[b2][:, cs:ncols], 1.0 / DA, msq[:, cs:ncols], ALU.mult, ALU.subtract, [("ps", b2), msq_r], [rsb_r])
            TS(rsb[:, cs:ncols], rsb[:, cs:ncols], EPS, None, ALU.add, None, [rsb_r], [rsb_r])
            ACT(rsb[:, cs:ncols], rsb[:, cs:ncols], AF.Sqrt, [rsb_r], [rsb_r])
            RCP(rsb[:, cs:ncols], rsb[:, cs:ncols], [rsb_r], [rsb_r])
            for c in range(8):
                TT(cv[:, c, cs:ncols], cv[:, c, cs:ncols], mean[:, cs:ncols], ALU.subtract, [("cv", c), ("cvs", c), mean_r, PH], [("cv", c)])
                TT(cv[:, c, cs:ncols], cv[:, c, cs:ncols], rsb[:, cs:ncols], ALU.mult, [("cv", c), rsb_r, PH], [("cv", c)])
                ACT(bT[:, c, cs:ncols], cv[:, c, cs:ncols], AF.Silu, [("cv", c), "colv", PH], [("bT", c)],
                    scale=colv[:, CV_LNBG + c:CV_LNBG + c + 1], bias=colv[:, CV_LNBB + c:CV_LNBB + c + 1])
            if has_s:
                cvall = [("cv", c) for c in range(8)]
                for half in range(2):
                    gn, gn_r = T()
                    CP(gn[:, 0:256].rearrange("p (c q t) -> p c q t", c=4, q=16), gTs[:, half * 4:half * 4 + 4, :, 30:34],
                       [("gTs", c) for c in range(8)] + [PH], [gn_r])
                    b = AB()
                    for c4 in range(4):
                        TR(ps[b][0:64, c4 * 128:(c4 + 1) * 128], gn[:, c4 * 64:(c4 + 1) * 64], idf[:], [gn_r, "idf"], [("ps", b)])
                    CP(gst[0:64, half * 512:(half + 1) * 512], ps[b][0:64, :], [("ps", b), PH], cvall)
                for q in range(16):
                    DMA("sp", ncs_o[q, 26:30, :], gst[4 * q:4 * q + 4, :], cvall + [PH], ["oncs1"], "oncs1")
                out_keys.append("oncs1")
            barrier()
            if stop == "B" and pi == len(passes) - 1:
                break

            q_reads = [("qT", ec) for ec in range(8)] + [PH]
            for (s, rows, c0) in slots:
                if rows != 128:
                    continue
                r0 = cs if (first_pass and s == 0) else 0
                nr = 128 - r0
                q0 = c0 + r0
                bs2 = [AB(), AB()]
                for h in range(4):
                    pvw = ps[bs2[h // 2]][:].rearrange("p (h m) -> p h m", h=2)
                    for e2 in range(2):
                        ec = 2 * h + e2
                        MM(pvw[0:nr, h % 2, :], qT[:, ec, q0:q0 + nr], KT[:, ec, :], e2 == 0, e2 == 1,
                           q_reads + ["KT"], [("ps", bs2[h // 2])])
                small_i[0] = 80
                mx, mx_r = SM(4)
                se, se_r = SM(4)
                for hh in range(2):
                    P.op("dve", lambda e, o_=mx[0:nr, hh * 2:hh * 2 + 2], i_=ps[bs2[hh]][0:nr, :].rearrange("p (h m) -> p h m", h=2):
                         e.reduce_max(out=o_, in_=i_, axis=AX.X), [("ps", bs2[hh])], [mx_r])
                TS(mx[0:nr], mx[0:nr], -1.0 / 16, None, ALU.mult, None, [mx_r], [mx_r])
                MSET(se, 0.0, [se_r])
                pm = Pm[s % 2]
                pmr = ("Pm", s % 2)
                for h in range(4):
                    pvw = ps[bs2[h // 2]][:].rearrange("p (h m) -> p h m", h=2)
                    ACT(pm[0:nr, h, :], pvw[0:nr, h % 2, :], AF.Exp, [("ps", bs2[h // 2]), mx_r, se_r, PH], [pmr, se_r],
                        scale=1.0 / 16, bias=mx[0:nr, h:h + 1], accum_out=se[0:nr, h:h + 1])
                RCP(se[0:nr], se[0:nr], [se_r], [se_r])
                TT(pm[0:nr], pm[0:nr], se[0:nr].unsqueeze(2).to_broadcast([nr, 4, 256]), ALU.mult, [pmr, se_r, PH], [pmr])
                b = AB()
                pvb = psb[b].rearrange("p (k t) -> p k t", k=8)
                for h in range(4):
                    for m in range(2):
                        TR(pvb[:, h * 2 + m, 0:nr], pm[0:nr, h, m * 128:(m + 1) * 128], idb[0:nr, 0:nr], [pmr, "idb", PH], [("ps", b)])
                pt = PT[s % 2]
                ptr = ("PT", s % 2)
                CP(pt[:, :, 0:nr], pvb[:, :, 0:nr], [("ps", b), PH], [ptr], eng="act")
                for eh in range(2):
                    b = AB()
                    pv4 = ps[b][:].rearrange("p (k t) -> p k t", k=4)
                    for e4 in range(4):
                        ec = eh * 4 + e4
                        h = ec // 2
                        for m in range(2):
                            MM(pv4[:, e4, 0:nr], Vb[:, m, ec * 128:(ec + 1) * 128], pt[:, h * 2 + m, 0:nr], m == 0, m == 1,
                               [ptr, "Vb", PH], [("ps", b)])
                    CP(cT[:, eh * 4:eh * 4 + 4, q0:q0 + nr], pv4[:, :, 0:nr], [("ps", b), PH], [("cT", eh * 4 + e) for e in range(4)],
                       eng="dve" if eh else "act")
            if has_s:
                c0s = ncp
                bpv = 0
                pvs = ps[bpv][:].rearrange("p (k t) -> p k t", k=8)
                def st_KD(q):
                    bf = q % 2
                    DMA("pool", Kseq[bf], ck[q].rearrange("(m p) e -> p m e", p=128), [PH], [("Kseq", bf)], f"ks{bf}")

                def st_TK(q):
                    bf = q % 2
                    for eh in range(2):
                        b = AB()
                        pvb = psb[b].rearrange("p (k m) -> p k m", k=4)
                        for e4 in range(4):
                            ec = eh * 4 + e4
                            for m in range(2):
                                TR(pvb[:, e4, m * 128:(m + 1) * 128], Kseq[bf][:, m, ec * 128:(ec + 1) * 128], idb[:],
                                   [("Kseq", bf), "idb", PH], [("ps", b)])
                        CP(KTs[bf][:, eh * 4:eh * 4 + 4, :], pvb, [("ps", b), PH], [("KTs", bf)], eng="act" if eh else "dve")
                    if q + 2 < 16:
                        st_KD(q + 2)

                def st_VD(q):
                    bf = q % 2
                    DMA("pool", Vseq[bf], cvv[q].rearrange("(m p) e -> p m e", p=128), [PH], [("Vseq", bf)], f"vq{bf}")

                def st_S(q):
                    bf = q % 2
                    b = AB()
                    psc = ps[b][:, 0:32].rearrange("p (m h t) -> p m h t", m=2, h=4)
                    for m in range(2):
                        for h in range(4):
                            for e2 in range(2):
                                ec = 2 * h + e2
                                MM(psc[:, m, h, :], KTs[bf][:, ec, m * 128:(m + 1) * 128], qT[:, ec, c0s + 4 * q:c0s + 4 * q + 4],
                                   e2 == 0, e2 == 1, [("KTs", bf), PH] + q_reads, [("ps", b)])
                    ACT(expT[:, :, q * 16:(q + 1) * 16], ps[b][:, 0:32].rearrange("p (m c) -> p m c", m=2), AF.Exp,
                        [("ps", b), PH], [("expT", q)], scale=1.0 / 16)

                def st_V(q):
                    bf = q % 2
                    for ec in range(8):
                        h = ec // 2
                        for m in range(2):
                            MM(pvs[:, ec, 4 * q:4 * q + 4], Vseq[bf][:, m, ec * 128:(ec + 1) * 128],
                               expT[:, m, q * 16 + h * 4:q * 16 + h * 4 + 4], m == 0, m == 1,
                               [("Vseq", bf), ("expT", q), PH], [("ps", bpv)])

                st_KD(0); st_VD(0); st_KD(1); st_VD(1)
                st_TK(0); st_TK(1)
                for q in range(16):
                    st_S(q)
                    if q + 2 < 16:
                        st_TK(q + 2)
                    st_V(q)
                    if q + 2 < 16:
                        st_VD(q + 2)
                b = AB()
                for m in range(2):
                    MM(ps[b][:, 0:256], ones_b[:], expT[:, m, :], m == 0, m == 1, [("expT", q) for q in range(16)] + ["ones_b", PH], [("ps", b)])
                rsum, rsum_r = T()
                RCP(rsum[:, 0:256], ps[b][:, 0:256], [("ps", b)], [rsum_r])
                rsv = rsum[:, 0:256].rearrange("p (q h t) -> p q h t", q=16, h=4)
                for ec in range(8):
                    TT(cT[:, ec, c0s:c0s + 64].rearrange("p (q t) -> p q t", t=4), pvs[:, ec, :].rearrange("p (q t) -> p q t", t=4),
                       rsv[:, :, ec // 2, :], ALU.mult, [("ps", bpv), rsum_r, PH], [("cT", ec)])
            barrier()
            if stop == "C" and pi == len(passes) - 1:
                break

            for i in range(8):
                for bi, (Wp, actT, aname) in enumerate(((w_pa, uT, "uT"), (w_pb, bT, "bT"), (w_pc, cT, "cT"))):
                    bo = lin_item(Wp, 0, i * 256, 256, 8, lambda k, a=actT: a[:, k, cs:ncols],
                                  [(aname, k) for k in range(8)] + [PH], ncols, cs=cs)
                    bg = lin_item(w_in, 0, 5120 + bi * 2048 + i * 256, 256, 16, hfn, hT_reads, ncols, cs=cs)
                    for h in range(2):
                        t, tr = T()
                        ACT(t[:, cs:ncols], ps[bg[h]][:, cs:ncols], AF.Sigmoid, [("ps", bg[h])], [tr])
                        mm, mm_r = macc[h], ("macc", h)
                        if bi == 0:
                            TT(mm[:, cs:ncols], ps[bo[h]][:, cs:ncols], t[:, cs:ncols], ALU.mult, [("ps", bo[h]), tr], [mm_r])
                        elif bi == 1:
                            TT(t[:, cs:ncols], ps[bo[h]][:, cs:ncols], t[:, cs:ncols], ALU.mult, [("ps", bo[h]), tr], [tr])
                            TT(mm[:, cs:ncols], mm[:, cs:ncols], t[:, cs:ncols], ALU.add, [mm_r, tr], [mm_r])
                        else:
                            TT(t[:, cs:ncols], ps[bo[h]][:, cs:ncols], t[:, cs:ncols], ALU.mult, [("ps", bo[h]), tr], [tr])
                            TT(mixT[:, 2 * i + h, cs:ncols], mm[:, cs:ncols], t[:, cs:ncols], ALU.add, [mm_r, tr, PH],
                               [("mixT", 2 * i + h)])

            if stop == "S3" and pi == len(passes) - 1:
                break
            def add_to_X(bank, fo):
                t, tr = T()
                CP(t[:, cs:ncols], ps[bank][:, cs:ncols], [("ps", bank)], [tr], eng="act")
                b = AB()
                for (s, rows, c0) in slots:
                    TR(ps[b][:rows, s * 128:(s + 1) * 128], t[:, c0:c0 + rows], idf[:], [tr, "idf"], [("ps", b)])
                xr = [("X", s) for s in range(ntp)]
                TT(X[:, 0:ntp, fo * 128:(fo + 1) * 128], X[:, 0:ntp, fo * 128:(fo + 1) * 128],
                   ps[b][:, 0:ncp].rearrange("p (s e) -> p s e", s=ntp), ALU.add, [("ps", b)] + xr, xr)
                if has_s:
                    TT(X[0:64, ntp, fo * 128:(fo + 1) * 128], X[0:64, ntp, fo * 128:(fo + 1) * 128],
                       ps[b][0:64, ncp:ncp + 128], ALU.add, [("ps", b), ("X", ntp)], [("X", ntp)])

            mix_reads = [("mixT", k) for k in range(16)] + [PH]
            for i in range(8):
                banks = lin_item(w_o, 0, i * 256, 256, 16, lambda k: mixT[:, k, cs:ncols], mix_reads, ncols, cs=cs)
                for h in range(2):
                    add_to_X(banks[h], 2 * i + h)
            barrier()

            if stop == "S4" and pi == len(passes) - 1:
                break
            norm_T([(X[:rows, s, :], rows, c0, ("X", s)) for (s, rows, c0) in slots], CV_GFFN, hT,
                   lambda i: ("hT", i), "n2")

            if has_s:
                for j in range(NJ):
                    if j % 16 == 0:
                        jn = min(16, NJ - j)
                        DMA("sp", sst[0:32, 0:jn * 128], sffn[:, j * 128:(j + jn) * 128], [PH], ["sfst"], "sfst")
                    if j % 4 == 0:
                        b = AB()
                    TR(ps[b][:, (j % 4) * 32:(j % 4) * 32 + 32], sst[0:32, (j % 16) * 128:(j % 16 + 1) * 128], idf[0:32, 0:32],
                       ["sfst", "idf", PH], [("ps", b)])
                    if j % 4 == 3 or j == NJ - 1:
                        n = j % 4 + 1
                        j0 = j - (j % 4)
                        CP(hffn[:, j0:j0 + n, :, :], ps[b][:, 0:n * 32].rearrange("p (j q t) -> p j q t", j=n, q=16),
                           [("ps", b), PH], ["hffn"])
            for i in range(22):
                ncw = 256 if i < 21 else 128
                ba = lin_item(w_up, 0, i * 256, ncw, 16, hfn, hT_reads, ncols, cs=cs)
                bg = lin_item(w_up, 0, DFF + i * 256, ncw, 16, hfn, hT_reads, ncols, cs=cs)
                for h in range(ncw // 128):
                    j = 2 * i + h
                    fa, fa_r = T()
                    fc, fc_r = T()
                    if first_pass:
                        MSET(fa[:, 0:2], 0.0, [fa_r])
                    else:
                        CP(fa[:, 0:2], carry_f[:, :, j], ["carry_f"], [fa_r])
                    ACT(fa[:, 2 + cs:2 + ncp], ps[ba[h]][:, cs:ncp], AF.Copy, [("ps", ba[h])], [fa_r])
                    if first_pass:
                        TS(fa[:, 128:130], fa[:, 128:130], flag[:, 0:1], None, ALU.mult, None, [fa_r, "flag"], [fa_r])
                    CP(carry_f[:, :, j], fa[:, ncp:ncp + 2], [fa_r], ["carry_f"])
                    w0 = colv[:, CV_FFW + j:CV_FFW + j + 1]
                    w1 = colv[:, CV_FFW + NJ + j:CV_FFW + NJ + j + 1]
                    w2 = colv[:, CV_FFW + 2 * NJ + j:CV_FFW + 2 * NJ + j + 1]
                    bb_ = colv[:, CV_FFB + j:CV_FFB + j + 1]
                    ACT(fc[:, cs:ncp], fa[:, cs:ncp], AF.Identity, [fa_r, "colv"], [fc_r], scale=w0, bias=bb_)
                    STT(fc[:, cs:ncp], fa[:, 1 + cs:1 + ncp], w1, fc[:, cs:ncp], ALU.mult, ALU.add, [fa_r, fc_r, "colv"], [fc_r])
                    STT(fc[:, cs:ncp], fa[:, 2 + cs:2 + ncp], w2, fc[:, cs:ncp], ALU.mult, ALU.add, [fa_r, fc_r, "colv"], [fc_r])
                    if has_s:
                        fs, fs_r = T()
                        fsv = fs[:, 0:96].rearrange("p (q t) -> p q t", t=6)
                        CP(fsv[:, :, 0:2], hffn[:, j, :, :], ["hffn", PH], [fs_r])
                        ACT(fsv[:, :, 2:6], ps[ba[h]][:, ncp:ncp + 64].rearrange("p (q t) -> p q t", t=4), AF.Copy,
                            [("ps", ba[h])], [fs_r])
                        CP(faSn[:, j, :].rearrange("p (q t) -> p q t", t=2), fsv[:, :, 4:6], [fs_r, PH], ["faSn"])
                        fcs = fc[:, ncp:ncp + 64].rearrange("p (q t) -> p q t", t=4)
                        ACT(fcs, fsv[:, :, 0:4], AF.Identity, [fs_r, "colv"], [fc_r], scale=w0, bias=bb_)
                        STT(fcs, fsv[:, :, 1:5], w1, fcs, ALU.mult, ALU.add, [fs_r, fc_r, "colv"], [fc_r])
                        STT(fcs, fsv[:, :, 2:6], w2, fcs, ALU.mult, ALU.add, [fs_r, fc_r, "colv"], [fc_r])
                    ACT(fc[:, cs:ncols], fc[:, cs:ncols], AF.Gelu_apprx_tanh, [fc_r], [fc_r])
                    TT(yT[:, j, cs:ncols], ps[bg[h]][:, cs:ncols], fc[:, cs:ncols], ALU.mult, [("ps", bg[h]), fc_r, PH], [("yT", j)])

            if stop == "S6" and pi == len(passes) - 1:
                break
            barrier()
            DMA("sp", gfin_b, bc_dram(gfin_h, D), [PH], ["gfin"], "gfin")
            y_reads = [("yT", j) for j in range(NJ)] + [PH]
            for i in range(8):
                banks = [LB(), LB()]
                for (r0, K) in ((0, 16), (16, 16), (32, 11)):
                    lin_item(w_down, r0, i * 256, 256, K, lambda k: yT[:, k, cs:ncols], y_reads, ncols, cs=cs,
                             banks=banks, first=(r0 == 0), last=(r0 == 32))
                for h in range(2):
                    add_to_X(banks[h], 2 * i + h)

            osl = [(s, rows, c0) for (s, rows, c0) in slots if not (first_pass and s == 0)]
            small_i[0] = 0
            ss, ss_r = SM(4)
            ms, ms_r = SM(4)
            rstd, rstd_r = SM(4)
            MSET(ss, 0.0, [ss_r])
            MSET(ms, 1.0, [ms_r])
            for (s, rows, c0) in osl:
                ACT(xs[:rows], X[:rows, s, :], AF.Square, [("X", s), ss_r], ["xs", ss_r], accum_out=ss[:rows, s:s + 1])
            TS(ms, ss, 1.0 / D, EPS, ALU.mult, ALU.add, [ss_r, ms_r], [ms_r])
            ACT(ms, ms, AF.Sqrt, [ms_r], [ms_r])
            RCP(rstd, ms, [ms_r], [rstd_r])
            for (s, rows, c0) in osl:
                STT(X[:rows, s, :], X[:rows, s, :], rstd[:rows, s:s + 1], gfin_b[:rows], ALU.mult, ALU.mult,
                    [("X", s), rstd_r, "gfin"], [("X", s)])
                if rows == 128:
                    t = ptiles[s]
                    DMA("sp", yp[(t - 1) * 128:t * 128, :], X[:, s, :], [("X", s)], [("X", s)], f"X{s}")
                else:
                    DMA("sp", ys, X[0:64, s, :], [("X", s)], [("X", s)], f"X{s}")
                if f"X{s}" not in out_keys:
                    out_keys.append(f"X{s}")
            barrier()
            if has_s:
                for j in range(NJ):
                    if j % 4 == 0:
                        b = AB()
                    TR(ps[b][0:32, (j % 4) * 128:(j % 4 + 1) * 128], faSn[:, j, :], idf[:], ["faSn", "idf", PH], [("ps", b)])
                    if j % 4 == 3 or j == NJ - 1:
                        n = j % 4 + 1
                        j0 = j - (j % 4)
                        CP(stg[0:32, j0 * 128:(j0 + n) * 128],
                           ps[b][0:32, 0:n * 128], [("ps", b), PH], ["stg_nfs"], eng="act" if (j // 4) % 2 else "dve")
                DMA("sp", nfs_o, stg[0:32, 0:DFF], ["stg_nfs", PH], ["onfs"], "onfs")
                out_keys.append("onfs")
            if has_s:
                barrier()

        for half in range(2 if "end" not in SKIP else 0):
            b = AB()
            for c4 in range(4):
                TR(ps[b][0:30, c4 * 128:(c4 + 1) * 128], carry_g[:, half * 4 + c4, :], idf[:], ["carry_g", "idf"], [("ps", b)])
            CP(stg[0:30, 5504 + half * 512:5504 + (half + 1) * 512], ps[b][0:30, :], [("ps", b), PH], ["stg_ncp"])
        if "end" not in SKIP:
            DMA("sp", ncp_o, stg[0:30, 5504:6528], ["stg_ncp", PH], ["oncp"], "oncp")
        out_keys.append("oncp")
        b = AB()
        if "end" not in SKIP:
          TR(ps[b][0:86, 0:128], carry_f[:].rearrange("p t j -> p (t j)"), idf[:], ["carry_f", "idf"], [("ps", b)])
        if "end" not in SKIP:
            CP(stg[0:86, 6528:6656], ps[b][0:86, 0:128], [("ps", b), PH], ["stg_nfp"])
            DMA("sp", nfp_o.rearrange("t (j p) -> (t j) p", p=128), stg[0:86, 6528:6656], ["stg_nfp", PH], ["onfp"], "onfp")
        out_keys.append("onfp")
        out_res = ["omk", "omv", "oncp", "onfp", "WsT", "WsTs", "colv", "lnvg", "lnvb", "gfin", "bsb", "flag"] + [("X", s) for s in range(4)]
        if any(hs for _, hs in passes):
            out_res += ["ovs", "oncs0", "oncs1", "onfs"]
        if debug:
            out_res += ["dbg1", "dbg2", "dbg3"]
        P.op("sp", None, out_res, ())
        P.finalize()
        P.emit()
    return nc


def _prep_inputs(inp):
    f = lambda a: np.ascontiguousarray(np.asarray(a, dtype=np.float32))
    xpr = f(inp["x_prompt"])
    xsa = f(inp["x_sample"])
    colsrc = np.zeros((CV_ROWS, 128), np.float32)
    colsrc[CV_GMIX:CV_GMIX + 16] = f(inp["g_mix"])[0].reshape(16, 128)
    colsrc[CV_GFFN:CV_GFFN + 16] = f(inp["g_ffn"])[0].reshape(16, 128)
    colsrc[CV_GMEM:CV_GMEM + 16] = f(inp["g_mem"])[0].reshape(16, 128)
    colsrc[CV_LNBG:CV_LNBG + 8] = f(inp["ln_b_g"])[0].reshape(8, 128)
    colsrc[CV_LNBB:CV_LNBB + 8] = f(inp["ln_b_b"])[0].reshape(8, 128)
    colsrc[CV_CONVB:CV_CONVB + 8] = f(inp["conv_b"])[0].reshape(8, 128)
    colsrc[CV_CONVW:CV_CONVW + 248] = f(inp["conv_w"])[0].reshape(31 * 8, 128)
    colsrc[CV_FFW:CV_FFW + 129] = f(inp["ffn_conv_w"])[0].reshape(3 * NJ, 128)
    colsrc[CV_FFB:CV_FFB + NJ] = f(inp["ffn_conv_b"])[0].reshape(NJ, 128)
    shared = dict(
        colsrc=colsrc,
        lnvg=f(inp["ln_v_g"]).reshape(1, DA), lnvb=f(inp["ln_v_b"]).reshape(1, DA),
        gfin=f(inp["g_final"]).reshape(1, D), bs=f(inp["b_s"]).reshape(1, 1024),
        ws=f(inp["w_s"])[0],
        idb=np.eye(128).astype(ml_dtypes.bfloat16), idf=np.eye(128, dtype=np.float32),
        tril=np.tril(np.ones((128, 128), np.float32)),
        w_in=f(inp["w_in"])[0], w_pa=f(inp["w_pa"])[0], w_pb=f(inp["w_pb"])[0], w_pc=f(inp["w_pc"])[0],
        w_o=f(inp["w_o"])[0], w_k=f(inp["w_k"])[0], w_v=f(inp["w_v"])[0], w_up=f(inp["w_up"])[0],
        w_down=f(inp["w_down"])[0],
    )
    ck = f(inp["cache_mem_k"])[0].reshape(128, 256, 1024)
    cv = f(inp["cache_mem_v"])[0].reshape(128, 256, 1024)
    sconv = f(inp["state_conv"])[0]
    sffn = f(inp["state_ffn_conv"])[0]
    mem = f(inp["mem_prompt"])
    in_maps = []
    for c in range(NCORES):
        b, half = c // 2, c % 2
        xp = np.zeros((9 * 128, D), np.float32)
        xp[128:] = xpr[b, half * 1024:(half + 1) * 1024]
        if half:
            xp[:128] = xpr[b, 896:1024]
        m = dict(shared)
        m.update(
            xp=xp, xsm=np.ascontiguousarray(xsa[c * 16:(c + 1) * 16].reshape(64, D)), mem=np.ascontiguousarray(mem[b]),
            ck=np.ascontiguousarray(ck[c * 16:(c + 1) * 16]), cvv=np.ascontiguousarray(cv[c * 16:(c + 1) * 16]),
            sconv=np.ascontiguousarray(sconv[c * 16:(c + 1) * 16]),
            sffn=np.ascontiguousarray(sffn[c * 16:(c + 1) * 16].reshape(32, DFF)),
            flag=np.full((128, 1), float(half), np.float32),
        )
        in_maps.append(m)
    return in_maps


_NC_CACHE = {}


def kernel(**inputs):
    in_maps = _prep_inputs(inputs)
    if "nc" not in _NC_CACHE:
        _NC_CACHE["nc"] = build_nc()
    nc = _NC_CACHE["nc"]
    res = run_bass_kernel_spmd(nc, in_maps, core_ids=list(range(NCORES)))
    R = res.results
    y_prompt = np.zeros((4, 2048, D), np.float32)
    y_sample = np.zeros((128, 4, D), np.float32)
    mk = np.zeros((1, 4, 256, 4, 256), np.float32)
    mv = np.zeros((1, 4, 256, 4, 256), np.float32)
    cvp = np.zeros((1, 4, 30, 1024), np.float32)
    ffp = np.zeros((1, 4, 2, DFF), np.float32)
    cvs = np.zeros((1, 128, 30, 1024), np.float32)
    ffs = np.zeros((1, 128, 2, DFF), np.float32)
    vs = np.zeros((1, 128, 4, DA), np.float32)
    for c in range(NCORES):
        b, half = c // 2, c % 2
        r = R[c]
        y_prompt[b, half * 1024:(half + 1) * 1024] = r["yp"]
        y_sample[c * 16:(c + 1) * 16] = r["ys"].reshape(16, 4, D)
        cvs[0, c * 16:(c + 1) * 16] = r["ncs"]
        ffs[0, c * 16:(c + 1) * 16] = r["nfs"].reshape(16, 2, DFF)
        vs[0, c * 16:(c + 1) * 16] = r["vs"].reshape(16, 4, DA)
        if half == 0:
            mk[0, b] = r["mk"].reshape(256, 4, 256)
            mv[0, b] = r["mv"].reshape(256, 4, 256)
        else:
            cvp[0, b] = r["ncp"]
            ffp[0, b] = r["nfp"]
    return (y_prompt, y_sample, mk, mv, cvp, ffp, cvs, ffs, vs)
```
